# Optimizing a Trainium2 kernel written in Bass

```python
import jax, jax.numpy as jnp
from jax import lax
import numpy as np

D_MODEL = 1024
BATCH = 8
SEQ = 4096
DEPTH = 2

N_Q_HEADS = 16
N_KV_HEADS = 4
HEAD_DIM = 64
GROUP = N_Q_HEADS // N_KV_HEADS
WINDOW = 128
BLOCK = 128
ATTN_W = N_Q_HEADS * HEAD_DIM
KV_W = N_KV_HEADS * HEAD_DIM
CONV_W = D_MODEL
CONV_K = 3
REC_W = D_MODEL
REC_HEADS = 4
REC_HEAD_DIM = REC_W // REC_HEADS
REC_CONV_K = 4
LRU_C = 8.0
N_BRANCH = 3
BRANCH_W = D_MODEL
SPLIT_SIZES = (ATTN_W, KV_W, KV_W, CONV_W, CONV_W, CONV_W, REC_W, REC_W, N_BRANCH * D_MODEL)
IN_COLS = ATTN_W + 2 * KV_W + 3 * CONV_W + 2 * REC_W + N_BRANCH * D_MODEL
N_KEYS = 128
N_EXPERTS = N_KEYS * N_KEYS
PEER_HEADS = 8
PEER_TOPK = 16
PEER_KEY_DIM = 256
PEER_HALF = PEER_KEY_DIM // 2
PEER_CHUNK = 128
PLE_DIM = 256
EPS = 1e-6
NEG_INF = -1e30

kernel_name = "hybrid_swa_conv_rglru_peer"


def rmsnorm(x, g):
    x32 = x.astype(jnp.float32)
    y = x32 * lax.rsqrt(jnp.mean(x32 * x32, axis=-1, keepdims=True) + EPS) * g.astype(jnp.float32)
    return y.astype(x.dtype)


def causal_depthwise_conv(x, w, b=None):
    k_width, ch = w.shape
    y = lax.conv_general_dilated(
        x, w[:, None, :].astype(x.dtype), window_strides=(1,), padding=[(k_width - 1, 0)],
        dimension_numbers=('NWC', 'WIO', 'NWC'), feature_group_count=ch)
    if b is not None:
        y = y + b.astype(y.dtype)
    return y


def sliding_window_attention(q, k, v, sinks):
    b, s = q.shape[0], q.shape[1]
    nb = s // BLOCK
    qb = q.reshape(b, nb, BLOCK, N_KV_HEADS, GROUP, HEAD_DIM)

    def band(t):
        tp = jnp.pad(t, ((0, 0), (BLOCK, 0), (0, 0), (0, 0)))
        prev = tp[:, :s].reshape(b, nb, BLOCK, N_KV_HEADS, HEAD_DIM)
        cur = t.reshape(b, nb, BLOCK, N_KV_HEADS, HEAD_DIM)
        return jnp.concatenate([prev, cur], axis=2)

    kb, vb = band(k), band(v)
    scores = jnp.einsum('bnqhgd,bnkhd->bnhgqk', qb, kb,
                        preferred_element_type=jnp.float32) * (HEAD_DIM ** -0.5)
    q_pos = jnp.arange(nb)[:, None, None] * BLOCK + jnp.arange(BLOCK)[None, :, None]
    k_pos = jnp.arange(nb)[:, None, None] * BLOCK - BLOCK + jnp.arange(2 * BLOCK)[None, None, :]
    diff = q_pos - k_pos
    mask = (diff >= 0) & (diff < WINDOW) & (k_pos >= 0)
    scores = jnp.where(mask[None, :, None, None], scores, NEG_INF)
    sink = sinks.astype(jnp.float32).reshape(1, 1, N_KV_HEADS, GROUP, 1, 1)
    m = jnp.maximum(scores.max(axis=-1, keepdims=True), sink)
    e = jnp.exp(scores - m)
    probs = e / (e.sum(axis=-1, keepdims=True) + jnp.exp(sink - m))
    out = jnp.einsum('bnhgqk,bnkhd->bnqhgd', probs.astype(v.dtype), vb)
    return out.reshape(b, s, ATTN_W)


def rg_lru(x, w_r, b_r, w_i, b_i, lam):
    b, s, _ = x.shape
    x32 = x.astype(jnp.float32)
    xh = x32.reshape(b, s, REC_HEADS, REC_HEAD_DIM)
    r = jax.nn.sigmoid(jnp.einsum('bshi,hij->bshj', xh, w_r.astype(jnp.float32)).reshape(b, s, REC_W)
                       + b_r.astype(jnp.float32))
    i = jax.nn.sigmoid(jnp.einsum('bshi,hij->bshj', xh, w_i.astype(jnp.float32)).reshape(b, s, REC_W)
                       + b_i.astype(jnp.float32))
    log_a = -LRU_C * r * jax.nn.softplus(-lam.astype(jnp.float32))
    a = jnp.exp(log_a)
    inp = jnp.sqrt(-jnp.expm1(2.0 * log_a)) * (i * x32)

    def combine(left, right):
        a1, b1 = left
        a2, b2 = right
        return a1 * a2, a2 * b1 + b2

    _, h = lax.associative_scan(combine, (a, inp), axis=1)
    return h.astype(x.dtype)


def peer(xn, w_q, sub_keys, u, v):
    b, s, d = xn.shape
    t = b * s
    xt = xn.reshape(t, d)
    q = (xt @ w_q).astype(jnp.float32).reshape(t, PEER_HEADS, 2, PEER_HALF)
    sc = jnp.einsum('thpc,pnc->thpn', q, sub_keys.astype(jnp.float32))
    top_s, top_i = lax.top_k(sc, PEER_TOPK)
    cand = top_s[:, :, 0, :, None] + top_s[:, :, 1, None, :]
    best_s, best_c = lax.top_k(cand.reshape(t, PEER_HEADS, PEER_TOPK * PEER_TOPK), PEER_TOPK)
    i1 = jnp.take_along_axis(top_i[:, :, 0], best_c // PEER_TOPK, axis=-1)
    i2 = jnp.take_along_axis(top_i[:, :, 1], best_c % PEER_TOPK, axis=-1)
    expert = i1 * N_KEYS + i2
    gate = jax.nn.softmax(best_s, axis=-1)
    nc = t // PEER_CHUNK

    def chunk_fn(args):
        xc, ec, gc = args
        act = jax.nn.gelu(jnp.einsum('cd,chkd->chk', xc, u[ec]))
        return jnp.einsum('chk,chkd->cd', (gc * act).astype(xc.dtype), v[ec])

    out = lax.map(chunk_fn, (xt.reshape(nc, PEER_CHUNK, d),
                             expert.reshape(nc, PEER_CHUNK, PEER_HEADS, PEER_TOPK),
                             gate.reshape(nc, PEER_CHUNK, PEER_HEADS, PEER_TOPK)))
    return out.reshape(b, s, d)


def setup_inputs(seed: int = 0) -> dict:
    key = jax.random.key(seed)
    ks = jax.random.split(key, 26)
    f32 = jnp.float32

    def nrm(k, shape, scale):
        return jax.random.normal(k, shape, f32) * scale

    a0 = jax.random.uniform(ks[12], (DEPTH, REC_W), f32, 0.9, 0.999) ** (1.0 / LRU_C)
    return {
        "x": nrm(ks[0], (BATCH, SEQ, D_MODEL), 1.0),
        "p": nrm(ks[1], (DEPTH, BATCH, SEQ, PLE_DIM), 1.0),
        "norm_mix": 1.0 + nrm(ks[2], (DEPTH, D_MODEL), 0.02),
        "w_in": nrm(ks[3], (DEPTH, D_MODEL, IN_COLS), D_MODEL ** -0.5),
        "attn_sinks": nrm(ks[4], (DEPTH, N_Q_HEADS), 0.5),
        "conv_w": nrm(ks[5], (DEPTH, CONV_K, CONV_W), CONV_K ** -0.5),
        "rec_conv_w": nrm(ks[6], (DEPTH, REC_CONV_K, REC_W), REC_CONV_K ** -0.5),
        "rec_conv_b": nrm(ks[7], (DEPTH, REC_W), 0.02),
        "w_rgate": nrm(ks[8], (DEPTH, REC_HEADS, REC_HEAD_DIM, REC_HEAD_DIM), REC_HEAD_DIM ** -0.5),
        "b_rgate": nrm(ks[9], (DEPTH, REC_W), 0.02),
        "w_igate": nrm(ks[10], (DEPTH, REC_HEADS, REC_HEAD_DIM, REC_HEAD_DIM), REC_HEAD_DIM ** -0.5),
        "b_igate": nrm(ks[11], (DEPTH, REC_W), 0.02),
        "lru_lambda": jnp.log(a0) - jnp.log1p(-a0),
        "w_branch": nrm(ks[13], (DEPTH, N_BRANCH, BRANCH_W, D_MODEL), BRANCH_W ** -0.5),
        "w_out": nrm(ks[14], (DEPTH, D_MODEL, D_MODEL), D_MODEL ** -0.5),
        "norm_ffn": 1.0 + nrm(ks[15], (DEPTH, D_MODEL), 0.02),
        "w_peer_q": nrm(ks[16], (DEPTH, D_MODEL, PEER_HEADS * PEER_KEY_DIM), D_MODEL ** -0.5),
        "peer_sub_keys": nrm(ks[17], (DEPTH, 2, N_KEYS, PEER_HALF), PEER_HALF ** -0.5),
        "peer_u": nrm(ks[18], (DEPTH, N_EXPERTS, D_MODEL), D_MODEL ** -0.5),
        "peer_v": nrm(ks[19], (DEPTH, N_EXPERTS, D_MODEL), (PEER_HEADS * PEER_TOPK) ** -0.5),
        "norm_ple": 1.0 + nrm(ks[20], (DEPTH, D_MODEL), 0.02),
        "w_ple_gate": nrm(ks[21], (DEPTH, D_MODEL, D_MODEL), D_MODEL ** -0.5),
        "w_ple_proj": nrm(ks[22], (DEPTH, PLE_DIM, D_MODEL), PLE_DIM ** -0.5),
        "norm_final": 1.0 + nrm(ks[23], (D_MODEL,), 0.02),
    }


def reference(x, p, norm_mix, w_in, attn_sinks, conv_w, rec_conv_w, rec_conv_b, w_rgate, b_rgate,
              w_igate, b_igate, lru_lambda, w_branch, w_out, norm_ffn, w_peer_q, peer_sub_keys,
              peer_u, peer_v, norm_ple, w_ple_gate, w_ple_proj, norm_final):
    b, s, d = x.shape
    split_points = np.cumsum(SPLIT_SIZES)[:-1].tolist()
    h = x
    for l in range(DEPTH):
        xn = rmsnorm(h, norm_mix[l])
        z = xn @ w_in[l]
        q, k, v, c_b, c_c, c_x, r_x, r_y, gates = jnp.split(z, split_points, axis=-1)
        attn = sliding_window_attention(q.reshape(b, s, N_Q_HEADS, HEAD_DIM),
                                        k.reshape(b, s, N_KV_HEADS, HEAD_DIM),
                                        v.reshape(b, s, N_KV_HEADS, HEAD_DIM), attn_sinks[l])
        conv = c_b * causal_depthwise_conv(c_c * c_x, conv_w[l])
        rec = jax.nn.gelu(r_y) * rg_lru(causal_depthwise_conv(r_x, rec_conv_w[l], rec_conv_b[l]),
                                        w_rgate[l], b_rgate[l], w_igate[l], b_igate[l], lru_lambda[l])
        branches = jnp.stack([attn, conv, rec], axis=0)
        y = jnp.einsum('nbsw,nwd->nbsd', branches, w_branch[l])
        g = jax.nn.sigmoid(gates.reshape(b, s, N_BRANCH, d))
        merged = jnp.einsum('bsnd,nbsd->bsd', g, y)
        h = h + merged @ w_out[l]
        h = h + peer(rmsnorm(h, norm_ffn[l]), w_peer_q[l], peer_sub_keys[l], peer_u[l], peer_v[l])
        ple_gate = jax.nn.sigmoid(rmsnorm(h, norm_ple[l]) @ w_ple_gate[l])
        h = h + ple_gate * (p[l] @ w_ple_proj[l])
    return rmsnorm(h, norm_final)
```

```python
import numpy as np
from contextlib import ExitStack
import concourse.bass as bass
import concourse.mybir as mybir
from concourse.bass_utils import run_bass_kernel_spmd

F32 = mybir.dt.float32
BF16 = mybir.dt.bfloat16
I32 = mybir.dt.int32
U32 = mybir.dt.uint32
AF = mybir.ActivationFunctionType
ALU = mybir.AluOpType
AX = mybir.AxisListType

D = 1024
L = 2
TS = 512
NEXP = 16384
NPL = 14
NWB = 3
NGB = 6
GC = 1.5957691216057308


class Buf:
    ALL = []

    def __init__(self, name, t):
        self.name = name
        self.t = t
        self.reset()
        Buf.ALL.append(self)

    def reset(self):
        self.last_w = None
        self.readers = {}
        self.dsem = None
        self.dcnt = 0
        self.dep = 0

    def __getitem__(self, k):
        return self.t[k]


class Sched:
    ENGS = ("pe", "act", "dve", "pool", "sp")
    EPOCH = 30000
    DEPOCH = 24000

    def __init__(self, nc, es, needed=None):
        self.nc = nc
        self.es = es
        self.record = needed is None
        self.needed = needed
        self.targets = {e: set() for e in self.ENGS}
        self.incidx = {e: {} for e in self.ENGS}
        self.ninc = {e: 0 for e in self.ENGS}
        self.q = {e: [] for e in self.ENGS}
        self.cnt = {e: 0 for e in self.ENGS}
        self.sems = {e: [] for e in self.ENGS}
        self.seen = {e: {} for e in self.ENGS}
        self.dma_sems = {}
        self.nsem = 0
        self.ninst = 0

    def _newsem(self, name):
        self.nsem += 1
        if self.record:
            return ("dummy", name)
        return self.es.enter_context(self.nc.semaphore(name))

    def _esem(self, e, k):
        ep = (k - 1) // self.EPOCH
        while len(self.sems[e]) <= ep:
            self.sems[e].append(self._newsem(f"s_{e}_{len(self.sems[e])}"))
        return self.sems[e][ep], k - ep * self.EPOCH

    def _res(self, key, val):
        if isinstance(key, tuple):
            return (self.dma_sems[key], val)
        if self.record:
            self.targets[key].add(val)
            return (None, val)
        return self._esem(key, self.incidx[key][val])

    def emit(self, e, fn, reads=(), writes=(), dma_out=None, dma_in=None, dma_out2=(), chain=False):
        def flat(lst):
            out = []
            for x in lst:
                if hasattr(x, "bufs"):
                    out.extend(x.bufs)
                else:
                    out.append(x)
            return out
        reads, writes = flat(reads), flat(writes)
        waits = {}

        def need(dep):
            if dep is None:
                return
            key, val = dep
            if waits.get(key, 0) < val:
                waits[key] = val

        for b in reads:
            need(b.last_w)
        for b in writes:
            need(b.last_w)
            for d in b.readers.values():
                need(d)
        if dma_out is not None and not chain:
            need(dma_out.last_w)
            for d in dma_out.readers.values():
                need(d)
        if dma_in is not None:
            need(dma_in.last_w)
        for b2 in dma_out2:
            need(b2.last_w)
            for d in b2.readers.values():
                need(d)
        wl = []
        for key, val in waits.items():
            if key == e and e == "pe":
                continue
            if self.seen[e].get(key, 0) >= val:
                continue
            self.seen[e][key] = val
            wl.append((key, val))
        self.ninst += 1
        is_dma = dma_out is not None or dma_in is not None
        if is_dma:
            b = dma_out if dma_out is not None else dma_in
            if b.dsem is None or b.dcnt + 16 > self.DEPOCH:
                b.dsem = self._newsem(f"d_{b.name}_{b.dep}")
                b.dep += 1
                b.dcnt = 0
            b.dcnt += 16
            dkey = ("dma", id(b), b.dep)
            self.dma_sems[dkey] = b.dsem
            dep_me = (dkey, b.dcnt)
            inc = (b.dsem, 16)
            if dma_out is not None:
                b.last_w = dep_me
                b.readers = {}
                for b2 in dma_out2:
                    b2.last_w = dep_me
                    b2.readers = {}
            else:
                b.readers[dkey] = dep_me
            for r in reads:
                r.readers[dkey] = dep_me
        else:
            self.cnt[e] += 1
            n = self.cnt[e]
            dep_me = (e, n)
            inc = None
            if (not self.record) and n in self.needed[e]:
                self.ninc[e] += 1
                self.incidx[e][n] = self.ninc[e]
                mysem, _ = self._esem(e, self.ninc[e])
                inc = (mysem, 1)
            for r in reads:
                r.readers[e] = dep_me
            for w in writes:
                w.last_w = dep_me
                w.readers = {}
        wres = [self._res(k, v) for k, v in wl]
        if self.record:
            return

        def run(engine):
            for s, v in wres:
                engine.wait_ge(s, v)
            ins = fn(engine)
            if inc is not None:
                ins.then_inc(inc[0], inc[1])
        self.q[e].append(run)

    def wait_all(self, e, bufs):
        wl = []
        for b in bufs:
            if b.last_w is not None:
                wl.append(b.last_w)
            wl.extend(b.readers.values())
        wres = [self._res(k, v) for k, v in wl]
        if self.record:
            return

        def run(engine):
            for s, v in wres:
                engine.wait_ge(s, v)
        self.q[e].append(run)

    def flush(self):
        q = self.q
        with self.nc.Block() as block:
            @block.tensor
            def _(eng):
                for f in q["pe"]:
                    f(eng)

            @block.scalar
            def _(eng):
                for f in q["act"]:
                    f(eng)

            @block.vector
            def _(eng):
                for f in q["dve"]:
                    f(eng)

            @block.gpsimd
            def _(eng):
                for f in q["pool"]:
                    f(eng)

            @block.sync
            def _(eng):
                for f in q["sp"]:
                    f(eng)


class DrySched:
    def emit(self, *a, **k):
        pass

    def wait_all(self, *a, **k):
        pass


def wlayout():
    out = []
    out.append((("kk",), 4096, 128))
    out.append((("vv",), 2048, 128))
    for b in range(2):
        out.append((("q", b), 4096, 128))
    for j in range(8):
        out.append((("cv", j), 3072, 128))
    for jh in range(4):
        out.append((("rc", jh), 4096, 128))
        out.append((("rg", jh), 1024, 128))
    for n in range(3):
        for jb in range(2):
            out.append((("gt", n, jb), 4096, 128))
            out.append((("br", n, jb), 4096, 128))
    for jb in range(2):
        out.append((("wo", jb), 4096, 128))
    for b in range(4):
        out.append((("pq", b), 4096, 128))
    for jb in range(2):
        out.append((("pg", jb), 4096, 128))
        out.append((("pp", jb), 1024, 128))
    return out


def woffsets():
    offs = {}
    o = 0
    for l in range(L):
        for key, F, parts in wlayout():
            offs[(l,) + key] = (o, F, parts)
            o += F
    return offs, o


def _kblk(w):
    K, N = w.shape
    return np.ascontiguousarray(w.reshape(K // 128, 128, N).transpose(1, 0, 2)).reshape(128, -1)


def host_weights(inp):
    offs, tot = woffsets()
    W = np.zeros((128, tot), np.float32)

    def put(key, arr):
        o, F, parts = offs[key]
        assert arr.shape == (128, F), (key, arr.shape, F)
        W[:, o:o + F] = arr

    for l in range(L):
        wi = inp["w_in"][l]
        k = wi[:, 1024:1280].reshape(D, 4, 1, 64)
        put((l, "kk"), _kblk(np.broadcast_to(k, (D, 4, 2, 64)).reshape(D, 512)))
        put((l, "vv"), _kblk(wi[:, 1280:1536]))
        for b in range(2):
            put((l, "q", b), _kblk(wi[:, 512 * b:512 * b + 512]))
        for j in range(8):
            cols = np.concatenate([wi[:, 1536 + 128 * j:1536 + 128 * j + 128],
                                   wi[:, 2560 + 128 * j:2560 + 128 * j + 128],
                                   wi[:, 3584 + 128 * j:3584 + 128 * j + 128]], axis=1)
            put((l, "cv", j), _kblk(cols))
        for jh in range(4):
            cols = np.concatenate([wi[:, 4608 + 256 * jh:4608 + 256 * jh + 256],
                                   wi[:, 5632 + 256 * jh:5632 + 256 * jh + 256]], axis=1)
            put((l, "rc", jh), _kblk(cols))
            g = np.stack([_kblk(inp["w_rgate"][l, jh]).reshape(128, 2, 256),
                          _kblk(inp["w_igate"][l, jh]).reshape(128, 2, 256)], axis=1)
            put((l, "rg", jh), g.reshape(128, 1024))
        for n in range(3):
            for jb in range(2):
                c0 = 6656 + n * 1024 + 512 * jb
                put((l, "gt", n, jb), _kblk(wi[:, c0:c0 + 512]))
                put((l, "br", n, jb), _kblk(inp["w_branch"][l, n][:, 512 * jb:512 * jb + 512]))
        for jb in range(2):
            put((l, "wo", jb), _kblk(inp["w_out"][l][:, 512 * jb:512 * jb + 512]))
        for b in range(4):
            put((l, "pq", b), _kblk(inp["w_peer_q"][l][:, 512 * b:512 * b + 512]))
        for jb in range(2):
            put((l, "pg", jb), _kblk(inp["w_ple_gate"][l][:, 512 * jb:512 * jb + 512]))
            put((l, "pp", jb), _kblk(inp["w_ple_proj"][l][:, 512 * jb:512 * jb + 512]))
    return W


def host_cols(inp):
    vecs = []
    for l in range(L):
        vecs += [inp["norm_mix"][l], inp["norm_ffn"][l], inp["norm_ple"][l],
                 inp["conv_w"][l, 0], inp["conv_w"][l, 1], inp["conv_w"][l, 2],
                 inp["rec_conv_w"][l, 0], inp["rec_conv_w"][l, 1], inp["rec_conv_w"][l, 2],
                 inp["rec_conv_w"][l, 3], inp["rec_conv_b"][l], inp["b_rgate"][l],
                 inp["b_igate"][l], inp["lru_lambda"][l]]
    vecs.append(inp["norm_final"])
    a = np.stack([np.asarray(v, np.float32) for v in vecs], 0)
    a = a.reshape(a.shape[0], 8, 128).transpose(2, 0, 1)
    return np.ascontiguousarray(a).reshape(128, -1)


def build_nc(S_LEN, dbg=False):
    NST = S_LEN // TS
    nc = bass.Bass("TRN2", target_bir_lowering=False)
    es = ExitStack()
    Buf.ALL = []
    offs, WTOT = woffsets()
    NCOL = (L * NPL + 1) * 8

    x_d = nc.dram_tensor("x", [S_LEN, D], F32, kind="ExternalInput").ap()
    p_d = nc.dram_tensor("p", [L, S_LEN, 256], F32, kind="ExternalInput").ap()
    w_d = nc.dram_tensor("wts", [128, WTOT], F32, kind="ExternalInput").ap()
    c_d = nc.dram_tensor("cols", [128, NCOL], F32, kind="ExternalInput").ap()
    s_d = nc.dram_tensor("sinks", [128, L * 16], F32, kind="ExternalInput").ap()
    k_d = nc.dram_tensor("keysT", [128, L * 2 * 128], F32, kind="ExternalInput").ap()
    pu_d = nc.dram_tensor("pu", [L * NEXP, D], F32, kind="ExternalInput").ap()
    pv_d = nc.dram_tensor("pv", [L * NEXP, D], F32, kind="ExternalInput").ap()
    o_d = nc.dram_tensor("out", [S_LEN, D], F32, kind="ExternalOutput").ap()
    pvb_d = nc.dram_tensor("pvb_scr", [L * NEXP, D], BF16).ap()
    ut_d = nc.dram_tensor("ut_scr", [L * 128 * 128, D], BF16).ap()
    ut_bl = [Buf(f"ut_b{l}", None) for l in range(L)]
    pub_b = Buf("pub_b", None)
    pvb_bl = [Buf(f"pvb_b{l}", None) for l in range(L)]
    dbg_outs = {}

    cnt = [0]

    def sb(shape, dt, name=None):
        cnt[0] += 1
        name = "sb_" + (name or f"t{cnt[0]}")
        return Buf(name, es.enter_context(nc.sbuf_tensor(name, shape, dt)))

    def psb(shape, dt, name):
        return Buf(name, es.enter_context(nc.psum_tensor(name, shape, dt)))

    hT = sb([128, 8, TS], F32, "hT")
    xnT = sb([128, 8, TS], BF16, "xnT")
    arena_t = es.enter_context(nc.sbuf_tensor("sb_arena", [128, 32768], BF16))
    big = Buf("big", arena_t[:, 0:8192].bitcast(F32).rearrange("p (c t) -> p c t", c=8))
    attnT = Buf("attnT", arena_t[:, 8192:12288].rearrange("p (c t) -> p c t", c=8))
    cvm = Buf("cvm", arena_t[:, 12288:16384].rearrange("p (c t) -> p c t", c=8))
    recT = Buf("recT", arena_t[:, 16384:20480].rearrange("p (c t) -> p c t", c=8))
    sx = Buf("sx", arena_t[:, 20480:32768])
    SALL = [big, attnT, cvm, recT, sx]
    S3 = arena_t[:, :].rearrange("p (t i) -> p t i", i=128)
    KT = [sb([128, 4, 128 + TS], BF16, f"KT{l}") for l in range(L)]
    VX = [sb([128, 5, 4, 192], BF16, f"VX{l}") for l in range(L)]
    wb_t = es.enter_context(nc.sbuf_tensor("sb_wball", [128, 6 * 2048], BF16))
    whalf = [Buf(f"wh{i}", wb_t[:, i * 2048:(i + 1) * 2048]) for i in range(6)]

    class WView:
        def __init__(self, s0, n):
            self.bufs = [whalf[s0 + k] for k in range(n)]
            self.ap = wb_t[:, s0 * 2048:(s0 + n) * 2048]

        def __getitem__(self, k):
            return self.ap[k]
    rstd = sb([128, TS], F32, "rstd")
    Bt = [sb([128, 520], F32, f"B{i}") for i in range(5)]
    xr = sb([128, 2, 520], F32, "xr")
    xc = sb([128, 2, TS], F32, "xc")
    xcb = sb([128, 2, TS], BF16, "xcb")
    qh2 = [sb([128, TS], BF16, f"qh{i}") for i in range(2)]
    Pb2 = [sb([128, 256], BF16, f"Pb{i}") for i in range(2)]
    pTs2 = [sb([128, 256], BF16, f"pTs{i}") for i in range(2)]
    sm2 = [[sb([128, 1], F32, f"sm{k}_{i}") for i in range(6)] for k in range(2)]
    conv_tail = sb([128, L, 8, 2], F32, "ctail")
    rec_tail = sb([128, L, 8, 3], F32, "rtail")
    lru = sb([128, L, 8], F32, "lru")
    tv = sb([128, 4, 16, 16], F32, "tv")
    ti = sb([128, 4, 16, 16], U32, "ti")
    tif = sb([128, 16, 16], F32, "tif")
    scs = [sb([128, 128], F32, f"scs{i}") for i in range(2)]
    sc2 = sb([128, 128], F32, "sc2")
    qtg = [sb([128, TS], F32, f"qtg{i}") for i in range(2)]
    cand = sb([128, 256], F32, "cand")
    cand2 = sb([128, 256], F32, "cand2")
    eqb = sb([128, 256], F32, "eqb")
    bs = sb([128, 8, 16], F32, "bs")
    bc = sb([128, 8, 16], U32, "bc")
    au = sb([128, 128], U32, "au")
    bu = sb([128, 128], U32, "bu")
    af = sb([128, 8, 16], F32, "af")
    bf = sb([128, 8, 16], F32, "bf")
    i1 = sb([128, 8, 16], F32, "i1")
    i2 = sb([128, 8, 16], F32, "i2")
    gw = sb([128, 8, 16], F32, "gw")
    gs8 = sb([128, 8], F32, "gs8")
    wgt = sb([128, 2], F32, "wgt")
    xtok = sb([128, D], F32, "xtok")
    acc = sb([128, D], F32, "acc")
    ohA = [sb([128, 4, 128], BF16, f"ohA{i}") for i in range(2)]
    ohB = [sb([128, 4, 128], BF16, f"ohB{i}") for i in range(2)]
    io128 = sb([128, 128], F32, "io128")
    eidx4 = wgt
    pT = sb([128, 2, TS], BF16, "pT")
    ident = sb([128, 128], F32, "ident")
    identb = sb([128, 128], BF16, "identb")
    onesm = sb([128, 128], F32, "onesm")
    mask = sb([128, 256], F32, "mask")
    mask0 = sb([128, 256], F32, "mask0")
    iov = cand
    io16 = sb([128, 16], F32, "io16")
    cols = sb([128, NCOL], F32, "cols")
    dcol = sb([128, L, 3, 8], F32, "dcol")
    sinks = sb([128, L * 16], F32, "sinks")
    keysT = sb([128, L * 2, 128], F32, "keysT")
    epsc = sb([128, 1], F32, "epsc")
    pbanks = [psb([128, 512], F32, f"pb{i}") for i in range(6)]
    pbig = psb([128, 1024], F32, "pbig")
    pidx = [0]
    ppbase = [0]

    def pp():
        n = 6 - ppbase[0]
        b = pbanks[ppbase[0] + pidx[0] % n]
        pidx[0] += 1
        return b

    def prog(S, worder):
        def mm(out, lhsT, rhs, start, stop, R, Wr):
            S.emit("pe", lambda e: e.matmul(out, lhsT=lhsT, rhs=rhs, start=start, stop=stop), reads=R, writes=Wr)

        def tr(out, in_, idn, R, Wr):
            S.emit("pe", lambda e: e.transpose(out, in_, idn), reads=R, writes=Wr)

        def act(out, in_, func, R, Wr, bias=0.0, scale=1.0, accum=None):
            S.emit("act", lambda e: e.activation(out=out, in_=in_, func=func, bias=bias, scale=scale, accum_out=accum),
                   reads=R, writes=Wr)

        def cpa(out, in_, R, Wr):
            S.emit("act", lambda e: e.copy(out=out, in_=in_), reads=R, writes=Wr)

        def cp(eng, out, in_, R, Wr):
            S.emit(eng, lambda e: e.tensor_copy(out=out, in_=in_), reads=R, writes=Wr)

        def tt(out, in0, in1, op, R, Wr, eng="dve"):
            S.emit(eng, lambda e: e.tensor_tensor(out=out, in0=in0, in1=in1, op=op), reads=R, writes=Wr)

        def ts(out, in0, s1, s2, op0, op1, R, Wr, eng="dve"):
            if s2 is None:
                S.emit(eng, lambda e: e.tensor_scalar(out=out, in0=in0, scalar1=s1, scalar2=None, op0=op0), reads=R, writes=Wr)
            else:
                S.emit(eng, lambda e: e.tensor_scalar(out=out, in0=in0, scalar1=s1, scalar2=s2, op0=op0, op1=op1),
                       reads=R, writes=Wr)

        def stt(out, in0, scalar, in1, op0, op1, R, Wr, accum=None):
            S.emit("dve", lambda e: e.scalar_tensor_tensor(out=out, in0=in0, scalar=scalar, in1=in1, op0=op0, op1=op1,
                                                           accum_out=accum), reads=R, writes=Wr)

        def recip(out, in_, R, Wr):
            S.emit("dve", lambda e: e.reciprocal(out=out, in_=in_), reads=R, writes=Wr)

        def red(out, in_, op, R, Wr):
            S.emit("dve", lambda e: e.tensor_reduce(out=out, in_=in_, axis=AX.X, op=op), reads=R, writes=Wr)

        def memset(eng, ap, val, Wr):
            S.emit(eng, lambda e: e.memset(ap, val), writes=Wr)

        def dma(eng, out, in_, dma_out=None, dma_in=None, R=(), dma_out2=(), chain=False):
            S.emit(eng, lambda e: e.dma_start(out=out, in_=in_), reads=R, dma_out=dma_out, dma_in=dma_in, dma_out2=dma_out2,
                   chain=chain)

        def gather(outb, table, idx_ap, tabbuf):
            S.emit("pool", lambda e: e.indirect_dma_start(out=outb[:, :], out_offset=None, in_=table,
                                                          in_offset=bass.IndirectOffsetOnAxis(ap=idx_ap, axis=0)),
                   reads=[eidx4, tabbuf], dma_out=outb)

        def dump(name, buf, ap, shape, dt=F32):
            if not dbg:
                return
            if name not in dbg_outs:
                dbg_outs[name] = nc.dram_tensor("dbg_" + name, list(shape), dt, kind="ExternalOutput").ap()
            dma("sp", dbg_outs[name], ap, dma_in=buf)

        wstate = {"i": 0, "issued": 0}
        wrec = []
        wslots = []
        if worder is not None:
            p_ = 0
            for key in worder:
                if key[1] == "ud":
                    wslots.append((p_ % 6, 1))
                    p_ += 1
                else:
                    if p_ % 2:
                        p_ += 1
                    wslots.append((p_ % 6, 2))
                    p_ += 2

        def wdead(m, i):
            return (m <= i - 3) if worder[m][1] == "ud" else (m <= i - 2)

        def wview(i):
            s0, n = wslots[i]
            return WView(s0, n)

        def wissue(i):
            key = worder[i]
            wv_ = wview(i)
            if key[1] == "ud":
                l_, i1_ = key[0], key[2]
                r0 = (l_ * 128 + i1_) * 128
                dma("sp", wv_[:, 0:1024], ut_d[r0:r0 + 128, :], dma_out=wv_.bufs[0], R=[ut_bl[l_]])
                r1 = l_ * NEXP + i1_ * 128
                dma("sp", wv_[:, 1024:2048], pvb_d[r1:r1 + 128, :], dma_out=wv_.bufs[0], R=[pvb_bl[l_]], chain=True)
                return
            o, F, parts = offs[key]
            dma("pool", wv_[:, 0:F], w_d[:, o:o + F], dma_out=wv_.bufs[0], dma_out2=wv_.bufs[1:])

        def wget(key):
            if worder is None:
                wrec.append(key)
                return WView(0, 2)
            i = wstate["i"]
            assert worder[i] == key, (worder[i], key)
            while wstate["issued"] < len(worder):
                k = wstate["issued"]
                if k > i + 3:
                    break
                s0, n = wslots[k]
                mine = set((s0 + q) % 6 for q in range(n))
                ok = True
                for m in range(max(0, k - 6), k):
                    if wdead(m, i):
                        continue
                    sm0, nm = wslots[m]
                    if mine & set((sm0 + q) % 6 for q in range(nm)):
                        ok = False
                        break
                if not ok:
                    if k <= i:
                        raise RuntimeError("weight ring deadlock")
                    break
                wissue(k)
                wstate["issued"] += 1
            wstate["i"] += 1
            return wview(i)

        def w3(wb, ncols, kc=8):
            return wb[:, 0:kc * ncols].rearrange("p (k n) -> p k n", k=kc)

        def colp(l, idx, c):
            o = ((l * NPL + idx) * 8 + c) if l < L else (L * NPL * 8 + c)
            return cols[:, o:o + 1]

        dma("sp", cols[:, :], c_d, dma_out=cols)
        dma("sp", sinks[:, :], s_d, dma_out=sinks)
        dma("sp", keysT[:, :, :], k_d.rearrange("p (a n) -> p a n", n=128), dma_out=keysT)
        S.emit("pool", lambda e: e.iota(iov[:, :], [[1, 256]], base=0, channel_multiplier=-1,
                                        allow_small_or_imprecise_dtypes=True), writes=[iov])
        S.emit("pool", lambda e: e.iota(io16[:, :], [[1, 16]], base=0, channel_multiplier=0,
                                        allow_small_or_imprecise_dtypes=True), writes=[io16])
        ts(ident[:, :], iov[:, 0:128], 0.0, None, ALU.is_equal, None, [iov], [ident])
        cp("dve", identb[:, :], ident[:, :], [ident], [identb])
        memset("pool", onesm[:, :], 1.0 / D, [onesm])
        memset("pool", epsc[:, :], 1e-6, [epsc])
        ts(mask[:, :], iov[:, :], 1.0, None, ALU.is_ge, None, [iov], [mask])
        ts(mask0[:, :], iov[:, :], 128.0, None, ALU.is_le, None, [iov], [mask0])
        tt(mask[:, :], mask[:, :], mask0[:, :], ALU.mult, [mask, mask0], [mask])
        ts(mask[:, :], mask[:, :], 30000.0, -30000.0, ALU.mult, ALU.add, [mask], [mask])
        cp("dve", mask0[:, :], mask[:, :], [mask], [mask0])
        memset("dve", mask0[:, 0:128], -30000.0, [mask0])
        for l in range(L):
            memset("pool", KT[l][:, :, :], 0.0, [KT[l]])
            memset("pool", VX[l][:, :, :, :], 0.0, [VX[l]])
        memset("pool", conv_tail[:, :, :, :], 0.0, [conv_tail])
        memset("pool", rec_tail[:, :, :, :], 0.0, [rec_tail])
        memset("pool", lru[:, :, :], 0.0, [lru])
        for l in range(L):
            o11 = (l * NPL + 11) * 8
            ts(dcol[:, l, 0, :], cols[:, o11:o11 + 8], -1.0, None, ALU.mult, None, [cols], [dcol])
            ts(dcol[:, l, 1, :], cols[:, o11 + 8:o11 + 16], -1.0, None, ALU.mult, None, [cols], [dcol])
            act(dcol[:, l, 2, :], cols[:, o11 + 16:o11 + 24], AF.Exp, [cols], [dcol], scale=-1.0)
            ts(dcol[:, l, 2, :], dcol[:, l, 2, :], 1.0, None, ALU.add, None, [dcol], [dcol])
            act(dcol[:, l, 2, :], dcol[:, l, 2, :], AF.Ln, [dcol], [dcol])
            ts(dcol[:, l, 2, :], dcol[:, l, 2, :], -8.0, None, ALU.mult, None, [dcol], [dcol])

        S.emit("pool", lambda e: e.iota(io128[:, :], [[1, 128]], base=0, channel_multiplier=0,
                                        allow_small_or_imprecise_dtypes=True), writes=[io128])
        def ut_prep(lq):
          for t in range(lq * 128, (lq + 1) * 128):
              sbf = (Bt[0], Bt[1], qtg[0], qtg[1])[t % 4]
              dbf = Bt[2 + t % 3]
              srcv = sbf[:, 0:512].bitcast(BF16)
              dstv = dbf[:, 0:512].bitcast(BF16)
              dma("pool", srcv, pu_d[t * 128:(t + 1) * 128, :], dma_out=sbf)
              bank = pp()
              bv = bank[:, 0:512].bitcast(BF16)
              for kc in range(8):
                  tr(bv[:, kc * 128:(kc + 1) * 128], srcv[:, kc * 128:(kc + 1) * 128], identb[:, :], [sbf, identb], [bank])
              if t % 2:
                  cpa(dstv, bv, [bank], [dbf])
              else:
                  cp("dve", dstv, bv, [bank], [dbf])
              dma("sp", ut_d[t * 128:(t + 1) * 128, :], dstv, dma_out=ut_bl[lq], R=[dbf])
              vbf = (xc, xr)[t % 2]
              vv = vbf[:, 0, 0:512].bitcast(BF16)
              dma("pool", vv, pv_d[t * 128:(t + 1) * 128, :], dma_out=vbf)
              dma("sp", pvb_d[t * 128:(t + 1) * 128, :], vv, dma_out=pvb_bl[lq], R=[vbf])

        def proj(bank, wv, col0, rhs_buf, R, kcs=8, M=128):
            for kc in range(kcs):
                mm(bank[0:M, :], wv[:, kc, col0:col0 + M], rhs_buf[:, kc, :], kc == 0, kc == kcs - 1, R, [bank])

        def norm(l, idx, want_f32):
            act(big[:, :, :], hT[:, :, :], AF.Square, [hT], [big])
            bank = pp()
            for c in range(8):
                mm(bank[:, :], onesm[:, :], big[:, c, :], c == 0, c == 7, [onesm, big], [bank])
            act(rstd[:, :], bank[:, :], AF.Ln, [bank, epsc], [rstd], bias=epsc[:, :])
            act(rstd[:, :], rstd[:, :], AF.Exp, [rstd], [rstd], scale=-0.5)
            for c in range(8):
                if want_f32:
                    stt(big[:, c, :], hT[:, c, :], colp(l, idx, c), rstd[:, :], ALU.mult, ALU.mult, [hT, cols, rstd], [big])
                else:
                    stt(xnT[:, c, :], hT[:, c, :], colp(l, idx, c), rstd[:, :], ALU.mult, ALU.mult, [hT, cols, rstd], [xnT])

        def sigmoid_from(bank_ap, Bx, R, bias=0.0):
            act(Bx[:, 0:TS], bank_ap, AF.Sigmoid, R + [Bx], [Bx], bias=bias)

        def load_x(st):
            bigv = big[:, :, :].rearrange("p c t -> p (c t)").rearrange("p (b d) -> p b d", b=4)
            dma("sp", bigv, x_d[st * TS:(st + 1) * TS, :].rearrange("(b p) d -> p b d", p=128), dma_out=big)
            for c in range(8):
                bank = pp()
                for tb in range(4):
                    tr(bank[:, tb * 128:(tb + 1) * 128], bigv[:, tb, c * 128:(c + 1) * 128], ident[:, :], [big, ident], [bank])
                cpa(hT[:, c, :], bank[:, :], [bank], [hT])

        def mixer(st, l):
            norm(l, 0, False)
            wb = wget((l, "kk"))
            wv = w3(wb, 512)
            for kvh in range(4):
                bank = pp()
                proj(bank, wv, kvh * 128, xnT, [wb, xnT])
                cpa(KT[l][:, kvh, 128:128 + TS], bank[:, :], [bank], [KT[l]])
            wb = wget((l, "vv"))
            wv = w3(wb, 256)
            for tb in range(4):
                bank = pp()
                for kc in range(8):
                    mm(bank[:, 0:256], xnT[:, kc, tb * 128:(tb + 1) * 128], wv[:, kc, :], kc == 0, kc == 7, [wb, xnT], [bank])
                cpa(VX[l][:, 1 + tb, :, 64:128], bank[:, 0:256].rearrange("p (h d) -> p h d", h=4), [bank], [VX[l]])
            for b in range(2):
                wb = wget((l, "q", b))
                wv = w3(wb, 512)
                for pr in range(4):
                    pair = b * 4 + pr
                    bank = pp()
                    proj(bank, wv, pr * 128, xnT, [wb, xnT])
                    qh = qh2[pair % 2]
                    cpa(qh[:, :], bank[:, :], [bank], [qh])
                    kvh = (2 * pair) // 4
                    for tb in range(4):
                        avp = pp()
                        for hh in range(2):
                            h = 2 * pair + hh
                            pb0 = hh * 64
                            scp = pp()
                            mm(scp[:, 0:256], qh[pb0:pb0 + 64, tb * 128:(tb + 1) * 128],
                               KT[l][pb0:pb0 + 64, kvh, tb * 128:tb * 128 + 256], True, True, [qh, KT[l]], [scp])
                            mk = mask0 if (st == 0 and tb == 0) else mask
                            sS, eS = Bt[2 * hh], Bt[2 * hh + 1]
                            Pb, pTs, sm = Pb2[hh], pTs2[hh], sm2[hh]
                            stt(sS[:, 0:256], scp[:, 0:256], 0.125, mk[:, :], ALU.mult, ALU.add, [scp, mk], [sS])
                            red(sm[0][:, :], sS[:, 0:256], ALU.max, [sS], [sm[0]])
                            sk = sinks[:, l * 16 + h:l * 16 + h + 1]
                            ts(sm[1][:, :], sm[0][:, :], sk, -1.0, ALU.max, ALU.mult, [sm[0], sinks], [sm[1]])
                            act(eS[:, 0:256], sS[:, 0:256], AF.Exp, [sS, sm[1]], [eS, sm[2]], bias=sm[1][:, :], accum=sm[2][:, :])
                            act(sm[3][:, :], sm[1][:, :], AF.Exp, [sm[1], sinks], [sm[3]], bias=sk)
                            tt(sm[4][:, :], sm[2][:, :], sm[3][:, :], ALU.add, [sm[2], sm[3]], [sm[4]])
                            recip(sm[5][:, :], sm[4][:, :], [sm[4]], [sm[5]])
                            ts(Pb[:, :], eS[:, 0:256], sm[5][:, :], None, ALU.mult, None, [eS, sm[5]], [Pb])
                            ptp = pp()
                            ptv = ptp[:, 0:128].bitcast(BF16)
                            tr(ptv[:, 0:128], Pb[:, 0:128], identb[:, :], [Pb, identb], [ptp])
                            tr(ptv[:, 128:256], Pb[:, 128:256], identb[:, :], [Pb, identb], [ptp])
                            cpa(pTs[:, :], ptv[:, :], [ptp], [pTs])
                            c0 = 64 if hh == 0 else 0
                            for kb in range(2):
                                mm(avp[:, 0:128], VX[l][:, tb + kb, kvh, c0:c0 + 128], pTs[:, kb * 128:(kb + 1) * 128],
                                   hh == 0 and kb == 0, hh == 1 and kb == 1, [VX[l], pTs], [avp])
                        cpa(attnT[:, pair, tb * 128:(tb + 1) * 128], avp[:, 0:128], [avp], [attnT])
            cp("pool", KT[l][:, :, 0:128], KT[l][:, :, TS:TS + 128], [KT[l]], [KT[l]])
            cp("pool", VX[l][:, 0, :, :], VX[l][:, 4, :, :], [VX[l]], [VX[l]])
            for j in range(8):
                wb = wget((l, "cv", j))
                wv = w3(wb, 384)
                pb_, pc_, px_ = pp(), pp(), pp()
                proj(pb_, wv, 0, xnT, [wb, xnT])
                proj(pc_, wv, 128, xnT, [wb, xnT])
                proj(px_, wv, 256, xnT, [wb, xnT])
                T0, PR, AC = Bt[0], Bt[1], Bt[2]
                cpa(T0[:, 0:TS], pc_[:, :], [pc_], [T0])
                cp("dve", PR[:, 0:2], conv_tail[:, l, j, :], [conv_tail], [PR])
                tt(PR[:, 2:2 + TS], T0[:, 0:TS], px_[:, :], ALU.mult, [T0, px_], [PR])
                ts(AC[:, 0:TS], PR[:, 0:TS], colp(l, 3, j), None, ALU.mult, None, [PR, cols], [AC])
                stt(AC[:, 0:TS], PR[:, 1:1 + TS], colp(l, 4, j), AC[:, 0:TS], ALU.mult, ALU.add, [PR, cols, AC], [AC])
                stt(AC[:, 0:TS], PR[:, 2:2 + TS], colp(l, 5, j), AC[:, 0:TS], ALU.mult, ALU.add, [PR, cols, AC], [AC])
                tt(cvm[:, j, :], AC[:, 0:TS], pb_[:, :], ALU.mult, [AC, pb_], [cvm])
                cp("dve", conv_tail[:, l, j, :], PR[:, TS:TS + 2], [PR], [conv_tail])
            if st == 0 and l == 0:
                dump("conv", cvm, cvm[:, :, :], [128, 8, TS], BF16)
            for jh in range(4):
                wb = wget((l, "rc", jh))
                wv = w3(wb, 512)
                wgb = wget((l, "rg", jh))
                wg = wgb[:, 0:1024].rearrange("p (g i n) -> p g i n", g=2, i=2)
                for cc in range(2):
                    ch = 2 * jh + cc
                    bank = pp()
                    proj(bank, wv, cc * 128, xnT, [wb, xnT])
                    cpa(xr[:, cc, 3:3 + TS], bank[:, :], [bank], [xr])
                    cp("dve", xr[:, cc, 0:3], rec_tail[:, l, ch, :], [rec_tail], [xr])
                    ts(xc[:, cc, :], xr[:, cc, 0:TS], colp(l, 6, ch), colp(l, 10, ch), ALU.mult, ALU.add, [xr, cols], [xc])
                    for k in range(1, 4):
                        stt(xc[:, cc, :], xr[:, cc, k:k + TS], colp(l, 6 + k, ch), xc[:, cc, :], ALU.mult, ALU.add,
                            [xr, cols, xc], [xc])
                    cp("dve", rec_tail[:, l, ch, :], xr[:, cc, TS:TS + 3], [xr], [rec_tail])
                    cpa(xcb[:, cc, :], xc[:, cc, :], [xc], [xcb])
                for oc in range(2):
                    ch = 2 * jh + oc
                    pr_, pi_, py_ = pp(), pp(), pp()
                    for ic in range(2):
                        mm(pr_[:, :], wg[:, 0, ic, oc * 128:(oc + 1) * 128], xcb[:, ic, :], ic == 0, ic == 1, [wgb, xcb], [pr_])
                    for ic in range(2):
                        mm(pi_[:, :], wg[:, 1, ic, oc * 128:(oc + 1) * 128], xcb[:, ic, :], ic == 0, ic == 1, [wgb, xcb], [pi_])
                    proj(py_, wv, (2 + oc) * 128, xnT, [wb, xnT])
                    B1, B2, B3, B4, B5 = Bt
                    sigmoid_from(pr_[:, :], B1, [pr_, cols], bias=colp(l, 11, ch))
                    sigmoid_from(pi_[:, :], B2, [pi_, cols], bias=colp(l, 12, ch))
                    act(B3[:, 0:TS], B1[:, 0:TS], AF.Exp, [B1, dcol], [B3], scale=dcol[:, l, 2, ch:ch + 1])
                    if st == 0 and l == 0 and ch == 0:
                        dump("xr", xr, xr[:, 0, :], [128, 520])
                        dump("r", B1, B1[:, 0:TS], [128, TS]); dump("i", B2, B2[:, 0:TS], [128, TS])
                        dump("a", B3, B3[:, 0:TS], [128, TS]); dump("xc", xc, xc[:, 0, :], [128, TS])
                        dump("dcol", dcol, dcol[:, 0, :, :], [128, 3, 8])
                    tt(B4[:, 0:TS], B3[:, 0:TS], B3[:, 0:TS], ALU.mult, [B3], [B4])
                    ts(B4[:, 0:TS], B4[:, 0:TS], -1.0, 1.0, ALU.mult, ALU.add, [B4], [B4])
                    ts(B4[:, 0:TS], B4[:, 0:TS], 1e-20, None, ALU.max, None, [B4], [B4])
                    act(B4[:, 0:TS], B4[:, 0:TS], AF.Ln, [B4], [B4])
                    act(B4[:, 0:TS], B4[:, 0:TS], AF.Exp, [B4], [B4], scale=0.5)
                    tt(B4[:, 0:TS], B4[:, 0:TS], B2[:, 0:TS], ALU.mult, [B4, B2], [B4])
                    tt(B4[:, 0:TS], B4[:, 0:TS], xc[:, oc, :], ALU.mult, [B4, xc], [B4])
                    stt(B4[:, 0:1], B3[:, 0:1], lru[:, l, ch:ch + 1], B4[:, 0:1], ALU.mult, ALU.add, [B3, lru, B4], [B4])
                    S.emit("dve", lambda e, B3=B3, B4=B4, B5=B5: e.tensor_tensor_scan(
                        out=B5[:, 0:TS], data0=B3[:, 0:TS], data1=B4[:, 0:TS], initial=0.0, op0=ALU.mult, op1=ALU.add),
                        reads=[B3, B4], writes=[B5])
                    cp("dve", lru[:, l, ch:ch + 1], B5[:, TS - 1:TS], [B5], [lru])
                    if st == 0 and l == 0 and ch == 0:
                        dump("inp", B4, B4[:, 0:TS], [128, TS]); dump("hs", B5, B5[:, 0:TS], [128, TS])
                    act(B1[:, 0:TS], py_[:, :], AF.Gelu_apprx_tanh, [py_, B1], [B1])
                    tt(recT[:, ch, :], B1[:, 0:TS], B5[:, 0:TS], ALU.mult, [B1, B5], [recT])
            branches = [attnT, cvm, recT]
            for n in range(3):
                for jb in range(2):
                    wgb = wget((l, "gt", n, jb))
                    wbb = wget((l, "br", n, jb))
                    wgv, wbv = w3(wgb, 512), w3(wbb, 512)
                    for jj in range(4):
                        j = 4 * jb + jj
                        G, Y = pp(), pp()
                        proj(G, wgv, jj * 128, xnT, [wgb, xnT])
                        proj(Y, wbv, jj * 128, branches[n], [wbb, branches[n]])
                        B1 = Bt[(j + n) % 5]
                        sigmoid_from(G[:, :], B1, [G])
                        if n == 0:
                            tt(big[:, j, :], B1[:, 0:TS], Y[:, :], ALU.mult, [B1, Y], [big])
                        elif n == 1:
                            tt(B1[:, 0:TS], B1[:, 0:TS], Y[:, :], ALU.mult, [B1, Y], [B1])
                            tt(big[:, j, :], big[:, j, :], B1[:, 0:TS], ALU.add, [big, B1], [big])
                        else:
                            tt(B1[:, 0:TS], B1[:, 0:TS], Y[:, :], ALU.mult, [B1, Y], [B1])
                            tt(cvm[:, j, :], big[:, j, :], B1[:, 0:TS], ALU.add, [big, B1], [cvm])
            for jb in range(2):
                wb = wget((l, "wo", jb))
                wv = w3(wb, 512)
                for jj in range(4):
                    j = 4 * jb + jj
                    Dp = pp()
                    proj(Dp, wv, jj * 128, cvm, [wb, cvm])
                    tt(hT[:, j, :], hT[:, j, :], Dp[:, :], ALU.add, [hT, Dp], [hT])

        def peer(st, l):
            norm(l, 1, False)
            i1T, i2T, gT = qtg[0], qtg[1], Bt[0]
            for b in range(4):
                wb = wget((l, "pq", b))
                wv = w3(wb, 512)
                for gg in range(4):
                    g = 4 * b + gg
                    pk = g % 2
                    Q = pp()
                    proj(Q, wv, gg * 128, xnT, [wb, xnT])
                    qt = qtg[g % 2]
                    cpa(qt[:, :], Q[:, :], [Q], [qt])
                    for tb in range(4):
                        scp = pp()
                        mm(scp[:, 0:128], qt[:, tb * 128:(tb + 1) * 128], keysT[:, l * 2 + pk, :], True, True, [qt, keysT], [scp])
                        sc = scs[(g * 4 + tb) % 2]
                        cpa(sc[:, :], scp[:, 0:128], [scp], [sc])
                        S.emit("dve", lambda e, sc=sc, tb=tb, g=g: e.max(out=tv[:, tb, g, 0:8], in_=sc[:, :]), reads=[sc], writes=[tv])
                        S.emit("dve", lambda e, sc=sc, tb=tb, g=g: e.max_index(out=ti[:, tb, g, 0:8], in_max=tv[:, tb, g, 0:8], in_values=sc[:, :]),
                               reads=[sc, tv], writes=[ti])
                        S.emit("dve", lambda e, sc=sc, tb=tb, g=g: e.match_replace(out=sc2[:, :], in_to_replace=tv[:, tb, g, 0:8], in_values=sc[:, :],
                                                                                imm_value=-1e30), reads=[sc, tv], writes=[sc2])
                        S.emit("dve", lambda e, tb=tb, g=g: e.max(out=tv[:, tb, g, 8:16], in_=sc2[:, :]), reads=[sc2], writes=[tv])
                        S.emit("dve", lambda e, tb=tb, g=g: e.max_index(out=ti[:, tb, g, 8:16], in_max=tv[:, tb, g, 8:16], in_values=sc2[:, :]),
                               reads=[sc2, tv], writes=[ti])
            for tb in range(4):
                cp("dve", tif[:, :, :], ti[:, tb, :, :], [ti], [tif])
                for h in range(8):
                    in0 = tv[:, tb, 2 * h, :].unsqueeze(2).to_broadcast([128, 16, 16])
                    in1 = tv[:, tb, 2 * h + 1, :].unsqueeze(1).to_broadcast([128, 16, 16])
                    tt(cand[:, :].rearrange("p (a b) -> p a b", a=16), in0, in1, ALU.add, [tv], [cand])
                    S.emit("dve", lambda e, h=h: e.max(out=bs[:, h, 0:8], in_=cand[:, :]), reads=[cand], writes=[bs])
                    S.emit("dve", lambda e, h=h: e.max_index(out=bc[:, h, 0:8], in_max=bs[:, h, 0:8], in_values=cand[:, :]),
                           reads=[cand, bs], writes=[bc])
                    S.emit("dve", lambda e, h=h: e.match_replace(out=cand2[:, :], in_to_replace=bs[:, h, 0:8], in_values=cand[:, :],
                                                                 imm_value=-1e30), reads=[cand, bs], writes=[cand2])
                    S.emit("dve", lambda e, h=h: e.max(out=bs[:, h, 8:16], in_=cand2[:, :]), reads=[cand2], writes=[bs])
                    S.emit("dve", lambda e, h=h: e.max_index(out=bc[:, h, 8:16], in_max=bs[:, h, 8:16], in_values=cand2[:, :]),
                           reads=[cand2, bs], writes=[bc])
                bcf = bc[:, :, :].rearrange("p h k -> p (h k)")
                S.emit("dve", lambda e, bcf=bcf: e.tensor_single_scalar(out=au[:, :], in_=bcf, scalar=4, op=ALU.arith_shift_right),
                       reads=[bc], writes=[au])
                S.emit("dve", lambda e, bcf=bcf: e.tensor_single_scalar(out=bu[:, :], in_=bcf, scalar=15, op=ALU.bitwise_and),
                       reads=[bc], writes=[bu])
                cp("dve", af[:, :, :].rearrange("p h k -> p (h k)"), au[:, :], [au], [af])
                cp("dve", bf[:, :, :].rearrange("p h k -> p (h k)"), bu[:, :], [bu], [bf])
                for h in range(8):
                    for (sel, src_g, dst) in ((af, 2 * h, i1), (bf, 2 * h + 1, i2)):
                        eq3 = eqb[:, :].rearrange("p (k a) -> p k a", k=16)
                        tt(eq3, io16[:, :].unsqueeze(1).to_broadcast([128, 16, 16]),
                           sel[:, h, :].unsqueeze(2).to_broadcast([128, 16, 16]), ALU.is_equal, [io16, sel], [eqb])
                        tt(eq3, eq3, tif[:, src_g, :].unsqueeze(1).to_broadcast([128, 16, 16]), ALU.mult, [eqb, tif], [eqb])
                        red(dst[:, h, :], eq3, ALU.add, [eqb], [dst])
                tt(gw[:, :, :], bs[:, :, :], bs[:, :, 0:1].to_broadcast([128, 8, 16]), ALU.subtract, [bs], [gw])
                act(gw[:, :, :], gw[:, :, :], AF.Exp, [gw], [gw])
                red(gs8[:, :], gw[:, :, :], ALU.add, [gw], [gs8])
                recip(gs8[:, :], gs8[:, :], [gs8], [gs8])
                tt(gw[:, :, :], gw[:, :, :], gs8[:, :].unsqueeze(2).to_broadcast([128, 8, 16]), ALU.mult, [gw, gs8], [gw])
                for (srcb, dstb) in ((i1, i1T), (i2, i2T), (gw, gT)):
                    bank = pp()
                    tr(bank[:, 0:128], srcb[:, :, :].rearrange("p h k -> p (h k)"), ident[:, :], [srcb, ident], [bank])
                    cpa(dstb[:, tb * 128:(tb + 1) * 128], bank[:, 0:128], [bank], [dstb])
            if st == 0 and l == 0:
                dump("i1T", i1T, i1T[:, 0:512], [128, 512])
                dump("i2T", i2T, i2T[:, 0:512], [128, 512])
                dump("gT", gT, gT[:, 0:512], [128, 512])

            def accap(c):
                if c < 4:
                    return pbanks[c // 2], pbanks[c // 2][:, (c % 2) * 256:(c % 2) * 256 + 256]
                return pbig, pbig[:, (c - 4) * 256:(c - 4) * 256 + 256]

            for hf in range(2):
                ppbase[0] = 0
                for sbk in range(64):
                    t0 = hf * 256 + sbk * 4
                    O2, O1 = ohA[sbk % 2], ohB[sbk % 2]
                    iob = io128[:, :].unsqueeze(1).to_broadcast([128, 4, 128])
                    tt(O2[:, :, :], iob, i2T[:, t0:t0 + 4].unsqueeze(2).to_broadcast([128, 4, 128]), ALU.is_equal,
                       [io128, i2T], [O2])
                    tt(O1[:, :, :], iob, i1T[:, t0:t0 + 4].unsqueeze(2).to_broadcast([128, 4, 128]), ALU.is_equal,
                       [io128, i1T], [O1])
                    tt(O1[:, :, :], O1[:, :, :], gT[:, t0:t0 + 4].unsqueeze(2).to_broadcast([128, 4, 128]), ALU.mult,
                       [O1, gT], [O1], eng="pool")
                    bank = pp()
                    for tq in range(4):
                        mm(bank[:, tq * 128:(tq + 1) * 128], O2[:, tq, :], O1[:, tq, :], True, True, [O2, O1], [bank])
                    tl = sbk * 4
                    outv = S3[:, tl:tl + 4, :]
                    inv = bank[:, :].rearrange("p (t i) -> p t i", t=4)
                    cpa(outv, inv, [bank], SALL)
                ppbase[0] = 2
                LAG = 2
                tinfo = {}

                def emitH(i1v):
                    wb = wget((l, "ud", i1v))
                    UTv = wb[:, 0:1024].rearrange("p (k n) -> p k n", k=8)
                    Vv = wb[:, 1024:2048]
                    Hb = pp()
                    for kc in range(8):
                        mm(Hb[:, 0:256], UTv[:, kc, :], xnT[:, kc, hf * 256:(hf + 1) * 256], kc == 0, kc == 7, [wb, xnT], [Hb])
                    geB, GtB = (Bt[1], xcb)[i1v % 2], Bt[2 + i1v % 3]
                    gev = geB[:, 0:128].bitcast(BF16) if i1v % 2 == 0 else xcb[:, 0, 0:256]
                    Gtv = GtB[:, 0:128].bitcast(BF16)
                    act(gev, Hb[:, 0:256], AF.Gelu_apprx_tanh, [Hb], [geB])
                    tt(Gtv, gev, S3[:, :, i1v], ALU.mult, [geB] + SALL, [GtB])
                    tinfo[i1v] = (wb, Vv, GtB, Gtv)

                def emitO(i1v):
                    wb, Vv, GtB, Gtv = tinfo.pop(i1v)
                    for c in range(8):
                        ab, aap = accap(c)
                        mm(aap, Vv[:, c * 128:(c + 1) * 128], Gtv, i1v == 0 and c % 2 == 0, i1v == 127, [wb, GtB], [ab])

                for i in range(128 + LAG):
                    if i < 128:
                        emitH(i)
                    if i >= LAG:
                        emitO(i - LAG)
                for c in range(8):
                    ab, aap = accap(c)
                    tt(hT[:, c, hf * 256:(hf + 1) * 256], hT[:, c, hf * 256:(hf + 1) * 256], aap, ALU.add, [hT, ab], [hT])
                ppbase[0] = 0

        def ple(st, l):
            norm(l, 2, False)
            ptok = acc
            ptv_ = acc[:, :].rearrange("p (b d) -> p b d", b=4)
            dma("sp", ptv_, p_d[l, st * TS:(st + 1) * TS, :].rearrange("(b p) d -> p b d", p=128), dma_out=ptok)
            for kc in range(2):
                bank = pp()
                for tb in range(4):
                    tr(bank[:, tb * 128:(tb + 1) * 128], ptv_[:, tb, kc * 128:(kc + 1) * 128], ident[:, :], [ptok, ident], [bank])
                cpa(pT[:, kc, :], bank[:, :], [bank], [pT])
            for jb in range(2):
                wgb = wget((l, "pg", jb))
                wpb = wget((l, "pp", jb))
                wgv, wpv = w3(wgb, 512), w3(wpb, 512, kc=2)
                for jj in range(4):
                    j = 4 * jb + jj
                    G, Pj = pp(), pp()
                    proj(G, wgv, jj * 128, xnT, [wgb, xnT])
                    proj(Pj, wpv, jj * 128, pT, [wpb, pT], kcs=2)
                    B1 = Bt[j % 5]
                    sigmoid_from(G[:, :], B1, [G])
                    tt(B1[:, 0:TS], B1[:, 0:TS], Pj[:, :], ALU.mult, [B1, Pj], [B1])
                    tt(hT[:, j, :], hT[:, j, :], B1[:, 0:TS], ALU.add, [hT, B1], [hT])

        def final(st):
            norm(L, 0, True)
            for tb in range(4):
                for c in range(8):
                    tr(pbig[:, c * 128:(c + 1) * 128], big[:, c, tb * 128:(tb + 1) * 128], ident[:, :], [big, ident], [pbig])
                cpa(xtok[:, :], pbig[:, :], [pbig], [xtok])
                r0 = st * TS + tb * 128
                dma("sp", o_d[r0:r0 + 128, :], xtok[:, :], dma_in=xtok)

        def dump_h(name):
            dump(name, hT, hT[:, :, :], [128, 8, TS])

        for st in range(NST):
            load_x(st)
            if st == 0:
                dump_h("h_in")
            for l in range(L):
                mixer(st, l)
                if st == 0:
                    ut_prep(l)
                if st == 0 and l == 0:
                    dump("xn", xnT, xnT[:, :, :], [128, 8, TS], BF16)
                    dump("attn", attnT, attnT[:, :, :], [128, 8, TS], BF16)
                    dump("rec", recT, recT[:, :, :], [128, 8, TS], BF16)
                    dump("mrg", cvm, cvm[:, :, :], [128, 8, TS], BF16)
                    dump_h("h_mix")
                peer(st, l)
                if st == 0 and l == 0:
                    dump_h("h_peer")
                ple(st, l)
                if st == 0 and l == 0:
                    dump_h("h_ple")
            final(st)
        S.wait_all("sp", [xtok, hT, big, cvm, attnT, recT, xnT, qtg[0], qtg[1], Bt[0], Bt[1], Bt[2], Bt[3], Bt[4], xr, xc, dcol])
        return wrec

    worder = prog(DrySched(), None)
    pidx[0] = 0
    ppbase[0] = 0
    Srec = Sched(nc, es, needed=None)
    prog(Srec, worder)
    for b_ in Buf.ALL:
        b_.reset()
    pidx[0] = 0
    ppbase[0] = 0
    S = Sched(nc, es, needed=Srec.targets)
    prog(S, worder)
    S.flush()
    es.close()
    return nc, S, dbg_outs


def prep_shared(inp):
    inp = {k: np.asarray(v) for k, v in inp.items()}
    sh = {}
    sh["wts"] = host_weights(inp)
    sh["cols"] = host_cols(inp)
    sh["sinks"] = np.ascontiguousarray(np.broadcast_to(inp["attn_sinks"].reshape(1, L * 16), (128, L * 16))).astype(np.float32)
    kT = inp["peer_sub_keys"].transpose(3, 0, 1, 2)
    sh["keysT"] = np.ascontiguousarray(kT).reshape(128, L * 2 * 128).astype(np.float32)
    sh["pu"] = np.ascontiguousarray(inp["peer_u"].reshape(L * NEXP, D))
    sh["pv"] = np.ascontiguousarray(inp["peer_v"].reshape(L * NEXP, D))
    return sh


def kernel(**inputs):
    x = np.asarray(inputs["x"])
    p = np.asarray(inputs["p"])
    B, S_LEN, _ = x.shape
    sh = prep_shared(inputs)
    nc, _, _ = build_nc(S_LEN)
    in_maps = []
    for b in range(B):
        m = dict(sh)
        m["x"] = np.ascontiguousarray(x[b])
        m["p"] = np.ascontiguousarray(p[:, b])
        in_maps.append(m)
    res = run_bass_kernel_spmd(nc, in_maps, core_ids=list(range(B)))
    return np.stack([res.results[b]["out"] for b in range(B)], 0).astype(np.float32)
```

```python
import numpy as np
from contextlib import ExitStack
import concourse.bass as bass
import concourse.mybir as mybir
from concourse.bass_utils import run_bass_kernel_spmd

F32 = mybir.dt.float32
BF16 = mybir.dt.bfloat16
I32 = mybir.dt.int32
U32 = mybir.dt.uint32
AF = mybir.ActivationFunctionType
ALU = mybir.AluOpType
AX = mybir.AxisListType

D = 1024
L = 2
TS = 512
NEXP = 16384
NPL = 14
NWB = 3
NGB = 6
GC = 1.5957691216057308


class Buf:
    ALL = []

    def __init__(self, name, t):
        self.name = name
        self.t = t
        self.reset()
        Buf.ALL.append(self)

    def reset(self):
        self.last_w = None
        self.readers = {}
        self.dsem = None
        self.dcnt = 0
        self.dep = 0

    def __getitem__(self, k):
        return self.t[k]


class Sched:
    ENGS = ("pe", "act", "dve", "pool", "sp")
    EPOCH = 30000
    DEPOCH = 24000

    def __init__(self, nc, es, needed=None):
        self.nc = nc
        self.es = es
        self.record = needed is None
        self.needed = needed
        self.targets = {e: set() for e in self.ENGS}
        self.incidx = {e: {} for e in self.ENGS}
        self.ninc = {e: 0 for e in self.ENGS}
        self.q = {e: [] for e in self.ENGS}
        self.cnt = {e: 0 for e in self.ENGS}
        self.sems = {e: [] for e in self.ENGS}
        self.seen = {e: {} for e in self.ENGS}
        self.dma_sems = {}
        self.nsem = 0
        self.ninst = 0

    def _newsem(self, name):
        self.nsem += 1
        if self.record:
            return ("dummy", name)
        return self.es.enter_context(self.nc.semaphore(name))

    def _esem(self, e, k):
        ep = (k - 1) // self.EPOCH
        while len(self.sems[e]) <= ep:
            self.sems[e].append(self._newsem(f"s_{e}_{len(self.sems[e])}"))
        return self.sems[e][ep], k - ep * self.EPOCH

    def _res(self, key, val):
        if isinstance(key, tuple):
            return (self.dma_sems[key], val)
        if self.record:
            self.targets[key].add(val)
            return (None, val)
        return self._esem(key, self.incidx[key][val])

    def emit(self, e, fn, reads=(), writes=(), dma_out=None, dma_in=None, dma_out2=(), chain=False):
        def flat(lst):
            out = []
            for x in lst:
                if hasattr(x, "bufs"):
                    out.extend(x.bufs)
                else:
                    out.append(x)
            return out
        reads, writes = flat(reads), flat(writes)
        waits = {}

        def need(dep):
            if dep is None:
                return
            key, val = dep
            if waits.get(key, 0) < val:
                waits[key] = val

        for b in reads:
            need(b.last_w)
        for b in writes:
            need(b.last_w)
            for d in b.readers.values():
                need(d)
        if dma_out is not None and not chain:
            need(dma_out.last_w)
            for d in dma_out.readers.values():
                need(d)
        if dma_in is not None:
            need(dma_in.last_w)
        for b2 in dma_out2:
            need(b2.last_w)
            for d in b2.readers.values():
                need(d)
        wl = []
        for key, val in waits.items():
            if key == e and e == "pe":
                continue
            if self.seen[e].get(key, 0) >= val:
                continue
            self.seen[e][key] = val
            wl.append((key, val))
        self.ninst += 1
        is_dma = dma_out is not None or dma_in is not None
        if is_dma:
            b = dma_out if dma_out is not None else dma_in
            if b.dsem is None or b.dcnt + 16 > self.DEPOCH:
                b.dsem = self._newsem(f"d_{b.name}_{b.dep}")
                b.dep += 1
                b.dcnt = 0
            b.dcnt += 16
            dkey = ("dma", id(b), b.dep)
            self.dma_sems[dkey] = b.dsem
            dep_me = (dkey, b.dcnt)
            inc = (b.dsem, 16)
            if dma_out is not None:
                b.last_w = dep_me
                b.readers = {}
                for b2 in dma_out2:
                    b2.last_w = dep_me
                    b2.readers = {}
            else:
                b.readers[dkey] = dep_me
            for r in reads:
                r.readers[dkey] = dep_me
        else:
            self.cnt[e] += 1
            n = self.cnt[e]
            dep_me = (e, n)
            inc = None
            if (not self.record) and n in self.needed[e]:
                self.ninc[e] += 1
                self.incidx[e][n] = self.ninc[e]
                mysem, _ = self._esem(e, self.ninc[e])
                inc = (mysem, 1)
            for r in reads:
                r.readers[e] = dep_me
            for w in writes:
                w.last_w = dep_me
                w.readers = {}
        wres = [self._res(k, v) for k, v in wl]
        if self.record:
            return

        def run(engine):
            for s, v in wres:
                engine.wait_ge(s, v)
            ins = fn(engine)
            if inc is not None:
                ins.then_inc(inc[0], inc[1])
        self.q[e].append(run)

    def wait_all(self, e, bufs):
        wl = []
        for b in bufs:
            if b.last_w is not None:
                wl.append(b.last_w)
            wl.extend(b.readers.values())
        wres = [self._res(k, v) for k, v in wl]
        if self.record:
            return

        def run(engine):
            for s, v in wres:
                engine.wait_ge(s, v)
        self.q[e].append(run)

    def flush(self):
        q = self.q
        with self.nc.Block() as block:
            @block.tensor
            def _(eng):
                for f in q["pe"]:
                    f(eng)

            @block.scalar
            def _(eng):
                for f in q["act"]:
                    f(eng)

            @block.vector
            def _(eng):
                for f in q["dve"]:
                    f(eng)

            @block.gpsimd
            def _(eng):
                for f in q["pool"]:
                    f(eng)

            @block.sync
            def _(eng):
                for f in q["sp"]:
                    f(eng)


class DrySched:
    def emit(self, *a, **k):
        pass

    def wait_all(self, *a, **k):
        pass


def wlayout():
    out = []
    out.append((("kk",), 4096, 128))
    out.append((("vv",), 2048, 128))
    for b in range(2):
        out.append((("q", b), 4096, 128))
    for j in range(8):
        out.append((("cv", j), 3072, 128))
    for jh in range(4):
        out.append((("rc", jh), 4096, 128))
        out.append((("rg", jh), 1024, 128))
    for n in range(3):
        for jb in range(2):
            out.append((("gt", n, jb), 4096, 128))
            out.append((("br", n, jb), 4096, 128))
    for jb in range(2):
        out.append((("wo", jb), 4096, 128))
    for b in range(4):
        out.append((("pq", b), 4096, 128))
    for jb in range(2):
        out.append((("pg", jb), 4096, 128))
        out.append((("pp", jb), 1024, 128))
    return out


def woffsets():
    offs = {}
    o = 0
    for l in range(L):
        for key, F, parts in wlayout():
            offs[(l,) + key] = (o, F, parts)
            o += F
    return offs, o


def _kblk(w):
    K, N = w.shape
    return np.ascontiguousarray(w.reshape(K // 128, 128, N).transpose(1, 0, 2)).reshape(128, -1)


def host_weights(inp):
    offs, tot = woffsets()
    W = np.zeros((128, tot), np.float32)

    def put(key, arr):
        o, F, parts = offs[key]
        assert arr.shape == (128, F), (key, arr.shape, F)
        W[:, o:o + F] = arr

    for l in range(L):
        wi = inp["w_in"][l]
        k = wi[:, 1024:1280].reshape(D, 4, 1, 64)
        put((l, "kk"), _kblk(np.broadcast_to(k, (D, 4, 2, 64)).reshape(D, 512)))
        put((l, "vv"), _kblk(wi[:, 1280:1536]))
        for b in range(2):
            put((l, "q", b), _kblk(wi[:, 512 * b:512 * b + 512]))
        for j in range(8):
            cols = np.concatenate([wi[:, 1536 + 128 * j:1536 + 128 * j + 128],
                                   wi[:, 2560 + 128 * j:2560 + 128 * j + 128],
                                   wi[:, 3584 + 128 * j:3584 + 128 * j + 128]], axis=1)
            put((l, "cv", j), _kblk(cols))
        for jh in range(4):
            cols = np.concatenate([wi[:, 4608 + 256 * jh:4608 + 256 * jh + 256],
                                   wi[:, 5632 + 256 * jh:5632 + 256 * jh + 256]], axis=1)
            put((l, "rc", jh), _kblk(cols))
            g = np.stack([_kblk(inp["w_rgate"][l, jh]).reshape(128, 2, 256),
                          _kblk(inp["w_igate"][l, jh]).reshape(128, 2, 256)], axis=1)
            put((l, "rg", jh), g.reshape(128, 1024))
        for n in range(3):
            for jb in range(2):
                c0 = 6656 + n * 1024 + 512 * jb
                put((l, "gt", n, jb), _kblk(wi[:, c0:c0 + 512]))
                put((l, "br", n, jb), _kblk(inp["w_branch"][l, n][:, 512 * jb:512 * jb + 512]))
        for jb in range(2):
            put((l, "wo", jb), _kblk(inp["w_out"][l][:, 512 * jb:512 * jb + 512]))
        for b in range(4):
            put((l, "pq", b), _kblk(inp["w_peer_q"][l][:, 512 * b:512 * b + 512]))
        for jb in range(2):
            put((l, "pg", jb), _kblk(inp["w_ple_gate"][l][:, 512 * jb:512 * jb + 512]))
            put((l, "pp", jb), _kblk(inp["w_ple_proj"][l][:, 512 * jb:512 * jb + 512]))
    return W


def host_cols(inp):
    vecs = []
    for l in range(L):
        vecs += [inp["norm_mix"][l], inp["norm_ffn"][l], inp["norm_ple"][l],
                 inp["conv_w"][l, 0], inp["conv_w"][l, 1], inp["conv_w"][l, 2],
                 inp["rec_conv_w"][l, 0], inp["rec_conv_w"][l, 1], inp["rec_conv_w"][l, 2],
                 inp["rec_conv_w"][l, 3], inp["rec_conv_b"][l], inp["b_rgate"][l],
                 inp["b_igate"][l], inp["lru_lambda"][l]]
    vecs.append(inp["norm_final"])
    a = np.stack([np.asarray(v, np.float32) for v in vecs], 0)
    a = a.reshape(a.shape[0], 8, 128).transpose(2, 0, 1)
    return np.ascontiguousarray(a).reshape(128, -1)


def build_nc(S_LEN, dbg=False):
    NST = S_LEN // TS
    nc = bass.Bass("TRN2", target_bir_lowering=False)
    es = ExitStack()
    Buf.ALL = []
    offs, WTOT = woffsets()
    NCOL = (L * NPL + 1) * 8

    x_d = nc.dram_tensor("x", [S_LEN, D], F32, kind="ExternalInput").ap()
    p_d = nc.dram_tensor("p", [L, S_LEN, 256], F32, kind="ExternalInput").ap()
    w_d = nc.dram_tensor("wts", [128, WTOT], F32, kind="ExternalInput").ap()
    c_d = nc.dram_tensor("cols", [128, NCOL], F32, kind="ExternalInput").ap()
    s_d = nc.dram_tensor("sinks", [128, L * 16], F32, kind="ExternalInput").ap()
    k_d = nc.dram_tensor("keysT", [128, L * 2 * 128], F32, kind="ExternalInput").ap()
    pu_d = nc.dram_tensor("pu", [L * NEXP, D], F32, kind="ExternalInput").ap()
    pv_d = nc.dram_tensor("pv", [L * NEXP, D], F32, kind="ExternalInput").ap()
    o_d = nc.dram_tensor("out", [S_LEN, D], F32, kind="ExternalOutput").ap()
    pvb_d = nc.dram_tensor("pvb_scr", [L * NEXP, D], BF16).ap()
    ut_d = nc.dram_tensor("ut_scr", [L * 128 * 128, D], BF16).ap()
    ut_bl = [Buf(f"ut_b{l}", None) for l in range(L)]
    pub_b = Buf("pub_b", None)
    pvb_bl = [Buf(f"pvb_b{l}", None) for l in range(L)]
    dbg_outs = {}

    cnt = [0]

    def sb(shape, dt, name=None):
        cnt[0] += 1
        name = "sb_" + (name or f"t{cnt[0]}")
        return Buf(name, es.enter_context(nc.sbuf_tensor(name, shape, dt)))

    def psb(shape, dt, name):
        return Buf(name, es.enter_context(nc.psum_tensor(name, shape, dt)))

    hT = sb([128, 8, TS], F32, "hT")
    xnT = sb([128, 8, TS], BF16, "xnT")
    arena_t = es.enter_context(nc.sbuf_tensor("sb_arena", [128, 32768], BF16))
    big = Buf("big", arena_t[:, 0:8192].bitcast(F32).rearrange("p (c t) -> p c t", c=8))
    attnT = Buf("attnT", arena_t[:, 8192:12288].rearrange("p (c t) -> p c t", c=8))
    cvm = Buf("cvm", arena_t[:, 12288:16384].rearrange("p (c t) -> p c t", c=8))
    recT = Buf("recT", arena_t[:, 16384:20480].rearrange("p (c t) -> p c t", c=8))
    sx = Buf("sx", arena_t[:, 20480:32768])
    SALL = [big, attnT, cvm, recT, sx]
    S3 = arena_t[:, :].rearrange("p (t i) -> p t i", i=128)
    KT = [sb([128, 4, 128 + TS], BF16, f"KT{l}") for l in range(L)]
    VX = [sb([128, 5, 4, 192], BF16, f"VX{l}") for l in range(L)]
    wb_t = es.enter_context(nc.sbuf_tensor("sb_wball", [128, 6 * 2048], BF16))
    whalf = [Buf(f"wh{i}", wb_t[:, i * 2048:(i + 1) * 2048]) for i in range(6)]

    class WView:
        def __init__(self, s0, n):
            self.bufs = [whalf[s0 + k] for k in range(n)]
            self.ap = wb_t[:, s0 * 2048:(s0 + n) * 2048]

        def __getitem__(self, k):
            return self.ap[k]
    rstd = sb([128, TS], F32, "rstd")
    Bt = [sb([128, 520], F32, f"B{i}") for i in range(5)]
    xr = sb([128, 2, 520], F32, "xr")
    xc = sb([128, 2, TS], F32, "xc")
    xcb = sb([128, 2, TS], BF16, "xcb")
    qh2 = [sb([128, TS], BF16, f"qh{i}") for i in range(2)]
    Pb2 = [sb([128, 256], BF16, f"Pb{i}") for i in range(2)]
    pTs2 = [sb([128, 256], BF16, f"pTs{i}") for i in range(2)]
    sm2 = [[sb([128, 1], F32, f"sm{k}_{i}") for i in range(6)] for k in range(2)]
    conv_tail = sb([128, L, 8, 2], F32, "ctail")
    rec_tail = sb([128, L, 8, 3], F32, "rtail")
    lru = sb([128, L, 8], F32, "lru")
    tv = sb([128, 4, 16, 16], F32, "tv")
    ti = sb([128, 4, 16, 16], U32, "ti")
    tif = sb([128, 16, 16], F32, "tif")
    scs = [sb([128, 128], F32, f"scs{i}") for i in range(2)]
    sc2 = sb([128, 128], F32, "sc2")
    qtg = [sb([128, TS], F32, f"qtg{i}") for i in range(2)]
    cand = sb([128, 256], F32, "cand")
    cand2 = sb([128, 256], F32, "cand2")
    eqb = sb([128, 256], F32, "eqb")
    bs = sb([128, 8, 16], F32, "bs")
    bc = sb([128, 8, 16], U32, "bc")
    au = sb([128, 128], U32, "au")
    bu = sb([128, 128], U32, "bu")
    af = sb([128, 8, 16], F32, "af")
    bf = sb([128, 8, 16], F32, "bf")
    i1 = sb([128, 8, 16], F32, "i1")
    i2 = sb([128, 8, 16], F32, "i2")
    gw = sb([128, 8, 16], F32, "gw")
    gs8 = sb([128, 8], F32, "gs8")
    wgt = sb([128, 2], F32, "wgt")
    xtok = sb([128, D], F32, "xtok")
    acc = sb([128, D], F32, "acc")
    ohA = [sb([128, 4, 128], BF16, f"ohA{i}") for i in range(2)]
    ohB = [sb([128, 4, 128], BF16, f"ohB{i}") for i in range(2)]
    io128 = sb([128, 128], F32, "io128")
    eidx4 = wgt
    pT = sb([128, 2, TS], BF16, "pT")
    ident = sb([128, 128], F32, "ident")
    identb = sb([128, 128], BF16, "identb")
    onesm = sb([128, 128], F32, "onesm")
    mask = sb([128, 256], F32, "mask")
    mask0 = sb([128, 256], F32, "mask0")
    iov = cand
    io16 = sb([128, 16], F32, "io16")
    cols = sb([128, NCOL], F32, "cols")
    dcol = sb([128, L, 3, 8], F32, "dcol")
    sinks = sb([128, L * 16], F32, "sinks")
    keysT = sb([128, L * 2, 128], F32, "keysT")
    epsc = sb([128, 1], F32, "epsc")
    pbanks = [psb([128, 512], F32, f"pb{i}") for i in range(6)]
    pbig = psb([128, 1024], F32, "pbig")
    pidx = [0]
    ppbase = [0]

    def pp():
        n = 6 - ppbase[0]
        b = pbanks[ppbase[0] + pidx[0] % n]
        pidx[0] += 1
        return b

    def prog(S, worder):
        def mm(out, lhsT, rhs, start, stop, R, Wr):
            S.emit("pe", lambda e: e.matmul(out, lhsT=lhsT, rhs=rhs, start=start, stop=stop), reads=R, writes=Wr)

        def tr(out, in_, idn, R, Wr):
            S.emit("pe", lambda e: e.transpose(out, in_, idn), reads=R, writes=Wr)

        def act(out, in_, func, R, Wr, bias=0.0, scale=1.0, accum=None):
            S.emit("act", lambda e: e.activation(out=out, in_=in_, func=func, bias=bias, scale=scale, accum_out=accum),
                   reads=R, writes=Wr)

        def cpa(out, in_, R, Wr):
            S.emit("act", lambda e: e.copy(out=out, in_=in_), reads=R, writes=Wr)

        def cp(eng, out, in_, R, Wr):
            S.emit(eng, lambda e: e.tensor_copy(out=out, in_=in_), reads=R, writes=Wr)

        def tt(out, in0, in1, op, R, Wr, eng="dve"):
            S.emit(eng, lambda e: e.tensor_tensor(out=out, in0=in0, in1=in1, op=op), reads=R, writes=Wr)

        def ts(out, in0, s1, s2, op0, op1, R, Wr, eng="dve"):
            if s2 is None:
                S.emit(eng, lambda e: e.tensor_scalar(out=out, in0=in0, scalar1=s1, scalar2=None, op0=op0), reads=R, writes=Wr)
            else:
                S.emit(eng, lambda e: e.tensor_scalar(out=out, in0=in0, scalar1=s1, scalar2=s2, op0=op0, op1=op1),
                       reads=R, writes=Wr)

        def stt(out, in0, scalar, in1, op0, op1, R, Wr, accum=None):
            S.emit("dve", lambda e: e.scalar_tensor_tensor(out=out, in0=in0, scalar=scalar, in1=in1, op0=op0, op1=op1,
                                                           accum_out=accum), reads=R, writes=Wr)

        def recip(out, in_, R, Wr):
            S.emit("dve", lambda e: e.reciprocal(out=out, in_=in_), reads=R, writes=Wr)

        def red(out, in_, op, R, Wr):
            S.emit("dve", lambda e: e.tensor_reduce(out=out, in_=in_, axis=AX.X, op=op), reads=R, writes=Wr)

        def memset(eng, ap, val, Wr):
            S.emit(eng, lambda e: e.memset(ap, val), writes=Wr)

        def dma(eng, out, in_, dma_out=None, dma_in=None, R=(), dma_out2=(), chain=False):
            S.emit(eng, lambda e: e.dma_start(out=out, in_=in_), reads=R, dma_out=dma_out, dma_in=dma_in, dma_out2=dma_out2,
                   chain=chain)

        def gather(outb, table, idx_ap, tabbuf):
            S.emit("pool", lambda e: e.indirect_dma_start(out=outb[:, :], out_offset=None, in_=table,
                                                          in_offset=bass.IndirectOffsetOnAxis(ap=idx_ap, axis=0)),
                   reads=[eidx4, tabbuf], dma_out=outb)

        def dump(name, buf, ap, shape, dt=F32):
            if not dbg:
                return
            if name not in dbg_outs:
                dbg_outs[name] = nc.dram_tensor("dbg_" + name, list(shape), dt, kind="ExternalOutput").ap()
            dma("sp", dbg_outs[name], ap, dma_in=buf)

        wstate = {"i": 0, "issued": 0}
        wrec = []
        wslots = []
        if worder is not None:
            p_ = 0
            for key in worder:
                if key[1] == "ud":
                    wslots.append((p_ % 6, 1))
                    p_ += 1
                else:
                    if p_ % 2:
                        p_ += 1
                    wslots.append((p_ % 6, 2))
                    p_ += 2

        def wdead(m, i):
            return (m <= i - 3) if worder[m][1] == "ud" else (m <= i - 2)

        def wview(i):
            s0, n = wslots[i]
            return WView(s0, n)

        def wissue(i):
            key = worder[i]
            wv_ = wview(i)
            if key[1] == "ud":
                l_, i1_ = key[0], key[2]
                r0 = (l_ * 128 + i1_) * 128
                dma("sp", wv_[:, 0:1024], ut_d[r0:r0 + 128, :], dma_out=wv_.bufs[0], R=[ut_bl[l_]])
                r1 = l_ * NEXP + i1_ * 128
                dma("sp", wv_[:, 1024:2048], pvb_d[r1:r1 + 128, :], dma_out=wv_.bufs[0], R=[pvb_bl[l_]], chain=True)
                return
            o, F, parts = offs[key]
            dma("pool", wv_[:, 0:F], w_d[:, o:o + F], dma_out=wv_.bufs[0], dma_out2=wv_.bufs[1:])

        def wget(key):
            if worder is None:
                wrec.append(key)
                return WView(0, 2)
            i = wstate["i"]
            assert worder[i] == key, (worder[i], key)
            while wstate["issued"] < len(worder):
                k = wstate["issued"]
                if k > i + 3:
                    break
                s0, n = wslots[k]
                mine = set((s0 + q) % 6 for q in range(n))
                ok = True
                for m in range(max(0, k - 6), k):
                    if wdead(m, i):
                        continue
                    sm0, nm = wslots[m]
                    if mine & set((sm0 + q) % 6 for q in range(nm)):
                        ok = False
                        break
                if not ok:
                    if k <= i:
                        raise RuntimeError("weight ring deadlock")
                    break
                wissue(k)
                wstate["issued"] += 1
            wstate["i"] += 1
            return wview(i)

        def w3(wb, ncols, kc=8):
            return wb[:, 0:kc * ncols].rearrange("p (k n) -> p k n", k=kc)

        def colp(l, idx, c):
            o = ((l * NPL + idx) * 8 + c) if l < L else (L * NPL * 8 + c)
            return cols[:, o:o + 1]

        dma("sp", cols[:, :], c_d, dma_out=cols)
        dma("sp", sinks[:, :], s_d, dma_out=sinks)
        dma("sp", keysT[:, :, :], k_d.rearrange("p (a n) -> p a n", n=128), dma_out=keysT)
        S.emit("pool", lambda e: e.iota(iov[:, :], [[1, 256]], base=0, channel_multiplier=-1,
                                        allow_small_or_imprecise_dtypes=True), writes=[iov])
        S.emit("pool", lambda e: e.iota(io16[:, :], [[1, 16]], base=0, channel_multiplier=0,
                                        allow_small_or_imprecise_dtypes=True), writes=[io16])
        ts(ident[:, :], iov[:, 0:128], 0.0, None, ALU.is_equal, None, [iov], [ident])
        cp("dve", identb[:, :], ident[:, :], [ident], [identb])
        memset("pool", onesm[:, :], 1.0 / D, [onesm])
        memset("pool", epsc[:, :], 1e-6, [epsc])
        ts(mask[:, :], iov[:, :], 1.0, None, ALU.is_ge, None, [iov], [mask])
        ts(mask0[:, :], iov[:, :], 128.0, None, ALU.is_le, None, [iov], [mask0])
        tt(mask[:, :], mask[:, :], mask0[:, :], ALU.mult, [mask, mask0], [mask])
        ts(mask[:, :], mask[:, :], 30000.0, -30000.0, ALU.mult, ALU.add, [mask], [mask])
        cp("dve", mask0[:, :], mask[:, :], [mask], [mask0])
        memset("dve", mask0[:, 0:128], -30000.0, [mask0])
        for l in range(L):
            memset("pool", KT[l][:, :, :], 0.0, [KT[l]])
            memset("pool", VX[l][:, :, :, :], 0.0, [VX[l]])
        memset("pool", conv_tail[:, :, :, :], 0.0, [conv_tail])
        memset("pool", rec_tail[:, :, :, :], 0.0, [rec_tail])
        memset("pool", lru[:, :, :], 0.0, [lru])
        for l in range(L):
            o11 = (l * NPL + 11) * 8
            ts(dcol[:, l, 0, :], cols[:, o11:o11 + 8], -1.0, None, ALU.mult, None, [cols], [dcol])
            ts(dcol[:, l, 1, :], cols[:, o11 + 8:o11 + 16], -1.0, None, ALU.mult, None, [cols], [dcol])
            act(dcol[:, l, 2, :], cols[:, o11 + 16:o11 + 24], AF.Exp, [cols], [dcol], scale=-1.0)
            ts(dcol[:, l, 2, :], dcol[:, l, 2, :], 1.0, None, ALU.add, None, [dcol], [dcol])
            act(dcol[:, l, 2, :], dcol[:, l, 2, :], AF.Ln, [dcol], [dcol])
            ts(dcol[:, l, 2, :], dcol[:, l, 2, :], -8.0, None, ALU.mult, None, [dcol], [dcol])

        S.emit("pool", lambda e: e.iota(io128[:, :], [[1, 128]], base=0, channel_multiplier=0,
                                        allow_small_or_imprecise_dtypes=True), writes=[io128])
        def ut_prep(lq):
          for t in range(lq * 128, (lq + 1) * 128):
              sbf = (Bt[0], Bt[1], qtg[0], qtg[1])[t % 4]
              dbf = Bt[2 + t % 3]
              srcv = sbf[:, 0:512].bitcast(BF16)
              dstv = dbf[:, 0:512].bitcast(BF16)
              dma("pool", srcv, pu_d[t * 128:(t + 1) * 128, :], dma_out=sbf)
              bank = pp()
              bv = bank[:, 0:512].bitcast(BF16)
              for kc in range(8):
                  tr(bv[:, kc * 128:(kc + 1) * 128], srcv[:, kc * 128:(kc + 1) * 128], identb[:, :], [sbf, identb], [bank])
              if t % 2:
                  cpa(dstv, bv, [bank], [dbf])
              else:
                  cp("dve", dstv, bv, [bank], [dbf])
              dma("sp", ut_d[t * 128:(t + 1) * 128, :], dstv, dma_out=ut_bl[lq], R=[dbf])
              vbf = (xc, xr)[t % 2]
              vv = vbf[:, 0, 0:512].bitcast(BF16)
              dma("pool", vv, pv_d[t * 128:(t + 1) * 128, :], dma_out=vbf)
              dma("sp", pvb_d[t * 128:(t + 1) * 128, :], vv, dma_out=pvb_bl[lq], R=[vbf])

        def proj(bank, wv, col0, rhs_buf, R, kcs=8, M=128):
            for kc in range(kcs):
                mm(bank[0:M, :], wv[:, kc, col0:col0 + M], rhs_buf[:, kc, :], kc == 0, kc == kcs - 1, R, [bank])

        def norm(l, idx, want_f32):
            act(big[:, :, :], hT[:, :, :], AF.Square, [hT], [big])
            bank = pp()
            for c in range(8):
                mm(bank[:, :], onesm[:, :], big[:, c, :], c == 0, c == 7, [onesm, big], [bank])
            act(rstd[:, :], bank[:, :], AF.Ln, [bank, epsc], [rstd], bias=epsc[:, :])
            act(rstd[:, :], rstd[:, :], AF.Exp, [rstd], [rstd], scale=-0.5)
            for c in range(8):
                if want_f32:
                    stt(big[:, c, :], hT[:, c, :], colp(l, idx, c), rstd[:, :], ALU.mult, ALU.mult, [hT, cols, rstd], [big])
                else:
                    stt(xnT[:, c, :], hT[:, c, :], colp(l, idx, c), rstd[:, :], ALU.mult, ALU.mult, [hT, cols, rstd], [xnT])

        def sigmoid_from(bank_ap, Bx, R, bias=0.0):
            act(Bx[:, 0:TS], bank_ap, AF.Sigmoid, R + [Bx], [Bx], bias=bias)

        def load_x(st):
            bigv = big[:, :, :].rearrange("p c t -> p (c t)").rearrange("p (b d) -> p b d", b=4)
            dma("sp", bigv, x_d[st * TS:(st + 1) * TS, :].rearrange("(b p) d -> p b d", p=128), dma_out=big)
            for c in range(8):
                bank = pp()
                for tb in range(4):
                    tr(bank[:, tb * 128:(tb + 1) * 128], bigv[:, tb, c * 128:(c + 1) * 128], ident[:, :], [big, ident], [bank])
                cpa(hT[:, c, :], bank[:, :], [bank], [hT])

        def mixer(st, l):
            norm(l, 0, False)
            wb = wget((l, "kk"))
            wv = w3(wb, 512)
            for kvh in range(4):
                bank = pp()
                proj(bank, wv, kvh * 128, xnT, [wb, xnT])
                cpa(KT[l][:, kvh, 128:128 + TS], bank[:, :], [bank], [KT[l]])
            wb = wget((l, "vv"))
            wv = w3(wb, 256)
            for tb in range(4):
                bank = pp()
                for kc in range(8):
                    mm(bank[:, 0:256], xnT[:, kc, tb * 128:(tb + 1) * 128], wv[:, kc, :], kc == 0, kc == 7, [wb, xnT], [bank])
                cpa(VX[l][:, 1 + tb, :, 64:128], bank[:, 0:256].rearrange("p (h d) -> p h d", h=4), [bank], [VX[l]])
            for b in range(2):
                wb = wget((l, "q", b))
                wv = w3(wb, 512)
                for pr in range(4):
                    pair = b * 4 + pr
                    bank = pp()
                    proj(bank, wv, pr * 128, xnT, [wb, xnT])
                    qh = qh2[pair % 2]
                    cpa(qh[:, :], bank[:, :], [bank], [qh])
                    kvh = (2 * pair) // 4
                    for tb in range(4):
                        avp = pp()

                        def chain(hh, avp=avp, qh=qh, pair=pair, tb=tb, kvh=kvh):
                            h = 2 * pair + hh
                            pb0 = hh * 64
                            scp = pp()
                            mm(scp[:, 0:256], qh[pb0:pb0 + 64, tb * 128:(tb + 1) * 128],
                               KT[l][pb0:pb0 + 64, kvh, tb * 128:tb * 128 + 256], True, True, [qh, KT[l]], [scp])
                            mk = mask0 if (st == 0 and tb == 0) else mask
                            sS, eS = Bt[2 * hh], Bt[2 * hh + 1]
                            Pb, pTs, sm = Pb2[hh], pTs2[hh], sm2[hh]
                            yield
                            stt(sS[:, 0:256], scp[:, 0:256], 0.125, mk[:, :], ALU.mult, ALU.add, [scp, mk], [sS])
                            yield
                            red(sm[0][:, :], sS[:, 0:256], ALU.max, [sS], [sm[0]])
                            yield
                            sk = sinks[:, l * 16 + h:l * 16 + h + 1]
                            ts(sm[1][:, :], sm[0][:, :], sk, -1.0, ALU.max, ALU.mult, [sm[0], sinks], [sm[1]])
                            yield
                            act(eS[:, 0:256], sS[:, 0:256], AF.Exp, [sS, sm[1]], [eS, sm[2]], bias=sm[1][:, :], accum=sm[2][:, :])
                            act(sm[3][:, :], sm[1][:, :], AF.Exp, [sm[1], sinks], [sm[3]], bias=sk)
                            yield
                            tt(sm[4][:, :], sm[2][:, :], sm[3][:, :], ALU.add, [sm[2], sm[3]], [sm[4]])
                            yield
                            recip(sm[5][:, :], sm[4][:, :], [sm[4]], [sm[5]])
                            yield
                            ts(Pb[:, :], eS[:, 0:256], sm[5][:, :], None, ALU.mult, None, [eS, sm[5]], [Pb])
                            yield
                            ptp = pp()
                            ptv = ptp[:, 0:128].bitcast(BF16)
                            tr(ptv[:, 0:128], Pb[:, 0:128], identb[:, :], [Pb, identb], [ptp])
                            tr(ptv[:, 128:256], Pb[:, 128:256], identb[:, :], [Pb, identb], [ptp])
                            yield
                            cpa(pTs[:, :], ptv[:, :], [ptp], [pTs])
                            yield
                            c0 = 64 if hh == 0 else 0
                            for kb in range(2):
                                mm(avp[:, 0:128], VX[l][:, tb + kb, kvh, c0:c0 + 128], pTs[:, kb * 128:(kb + 1) * 128],
                                   hh == 0 and kb == 0, hh == 1 and kb == 1, [VX[l], pTs], [avp])

                        gens = [chain(0), chain(1)]
                        while gens:
                            for g_ in list(gens):
                                try:
                                    next(g_)
                                except StopIteration:
                                    gens.remove(g_)
                        cpa(attnT[:, pair, tb * 128:(tb + 1) * 128], avp[:, 0:128], [avp], [attnT])
            cp("pool", KT[l][:, :, 0:128], KT[l][:, :, TS:TS + 128], [KT[l]], [KT[l]])
            cp("pool", VX[l][:, 0, :, :], VX[l][:, 4, :, :], [VX[l]], [VX[l]])
            for j in range(8):
                wb = wget((l, "cv", j))
                wv = w3(wb, 384)
                pb_, pc_, px_ = pp(), pp(), pp()
                proj(pb_, wv, 0, xnT, [wb, xnT])
                proj(pc_, wv, 128, xnT, [wb, xnT])
                proj(px_, wv, 256, xnT, [wb, xnT])
                T0, PR, AC = Bt[0], Bt[1], Bt[2]
                cpa(T0[:, 0:TS], pc_[:, :], [pc_], [T0])
                cp("dve", PR[:, 0:2], conv_tail[:, l, j, :], [conv_tail], [PR])
                tt(PR[:, 2:2 + TS], T0[:, 0:TS], px_[:, :], ALU.mult, [T0, px_], [PR])
                ts(AC[:, 0:TS], PR[:, 0:TS], colp(l, 3, j), None, ALU.mult, None, [PR, cols], [AC])
                stt(AC[:, 0:TS], PR[:, 1:1 + TS], colp(l, 4, j), AC[:, 0:TS], ALU.mult, ALU.add, [PR, cols, AC], [AC])
                stt(AC[:, 0:TS], PR[:, 2:2 + TS], colp(l, 5, j), AC[:, 0:TS], ALU.mult, ALU.add, [PR, cols, AC], [AC])
                tt(cvm[:, j, :], AC[:, 0:TS], pb_[:, :], ALU.mult, [AC, pb_], [cvm])
                cp("dve", conv_tail[:, l, j, :], PR[:, TS:TS + 2], [PR], [conv_tail])
            if st == 0 and l == 0:
                dump("conv", cvm, cvm[:, :, :], [128, 8, TS], BF16)
            for jh in range(4):
                wb = wget((l, "rc", jh))
                wv = w3(wb, 512)
                wgb = wget((l, "rg", jh))
                wg = wgb[:, 0:1024].rearrange("p (g i n) -> p g i n", g=2, i=2)
                for cc in range(2):
                    ch = 2 * jh + cc
                    bank = pp()
                    proj(bank, wv, cc * 128, xnT, [wb, xnT])
                    cpa(xr[:, cc, 3:3 + TS], bank[:, :], [bank], [xr])
                    cp("dve", xr[:, cc, 0:3], rec_tail[:, l, ch, :], [rec_tail], [xr])
                    ts(xc[:, cc, :], xr[:, cc, 0:TS], colp(l, 6, ch), colp(l, 10, ch), ALU.mult, ALU.add, [xr, cols], [xc])
                    for k in range(1, 4):
                        stt(xc[:, cc, :], xr[:, cc, k:k + TS], colp(l, 6 + k, ch), xc[:, cc, :], ALU.mult, ALU.add,
                            [xr, cols, xc], [xc])
                    cp("dve", rec_tail[:, l, ch, :], xr[:, cc, TS:TS + 3], [xr], [rec_tail])
                    cpa(xcb[:, cc, :], xc[:, cc, :], [xc], [xcb])
                for oc in range(2):
                    ch = 2 * jh + oc
                    pr_, pi_, py_ = pp(), pp(), pp()
                    for ic in range(2):
                        mm(pr_[:, :], wg[:, 0, ic, oc * 128:(oc + 1) * 128], xcb[:, ic, :], ic == 0, ic == 1, [wgb, xcb], [pr_])
                    for ic in range(2):
                        mm(pi_[:, :], wg[:, 1, ic, oc * 128:(oc + 1) * 128], xcb[:, ic, :], ic == 0, ic == 1, [wgb, xcb], [pi_])
                    proj(py_, wv, (2 + oc) * 128, xnT, [wb, xnT])
                    B1, B2, B3, B4, B5 = Bt
                    sigmoid_from(pr_[:, :], B1, [pr_, cols], bias=colp(l, 11, ch))
                    sigmoid_from(pi_[:, :], B2, [pi_, cols], bias=colp(l, 12, ch))
                    act(B3[:, 0:TS], B1[:, 0:TS], AF.Exp, [B1, dcol], [B3], scale=dcol[:, l, 2, ch:ch + 1])
                    if st == 0 and l == 0 and ch == 0:
                        dump("xr", xr, xr[:, 0, :], [128, 520])
                        dump("r", B1, B1[:, 0:TS], [128, TS]); dump("i", B2, B2[:, 0:TS], [128, TS])
                        dump("a", B3, B3[:, 0:TS], [128, TS]); dump("xc", xc, xc[:, 0, :], [128, TS])
                        dump("dcol", dcol, dcol[:, 0, :, :], [128, 3, 8])
                    tt(B4[:, 0:TS], B3[:, 0:TS], B3[:, 0:TS], ALU.mult, [B3], [B4])
                    ts(B4[:, 0:TS], B4[:, 0:TS], -1.0, 1.0, ALU.mult, ALU.add, [B4], [B4])
                    ts(B4[:, 0:TS], B4[:, 0:TS], 1e-20, None, ALU.max, None, [B4], [B4])
                    act(B4[:, 0:TS], B4[:, 0:TS], AF.Ln, [B4], [B4])
                    act(B4[:, 0:TS], B4[:, 0:TS], AF.Exp, [B4], [B4], scale=0.5)
                    tt(B4[:, 0:TS], B4[:, 0:TS], B2[:, 0:TS], ALU.mult, [B4, B2], [B4])
                    tt(B4[:, 0:TS], B4[:, 0:TS], xc[:, oc, :], ALU.mult, [B4, xc], [B4])
                    stt(B4[:, 0:1], B3[:, 0:1], lru[:, l, ch:ch + 1], B4[:, 0:1], ALU.mult, ALU.add, [B3, lru, B4], [B4])
                    S.emit("dve", lambda e, B3=B3, B4=B4, B5=B5: e.tensor_tensor_scan(
                        out=B5[:, 0:TS], data0=B3[:, 0:TS], data1=B4[:, 0:TS], initial=0.0, op0=ALU.mult, op1=ALU.add),
                        reads=[B3, B4], writes=[B5])
                    cp("dve", lru[:, l, ch:ch + 1], B5[:, TS - 1:TS], [B5], [lru])
                    if st == 0 and l == 0 and ch == 0:
                        dump("inp", B4, B4[:, 0:TS], [128, TS]); dump("hs", B5, B5[:, 0:TS], [128, TS])
                    act(B1[:, 0:TS], py_[:, :], AF.Gelu_apprx_tanh, [py_, B1], [B1])
                    tt(recT[:, ch, :], B1[:, 0:TS], B5[:, 0:TS], ALU.mult, [B1, B5], [recT])
            branches = [attnT, cvm, recT]
            for n in range(3):
                for jb in range(2):
                    wgb = wget((l, "gt", n, jb))
                    wbb = wget((l, "br", n, jb))
                    wgv, wbv = w3(wgb, 512), w3(wbb, 512)
                    for jj in range(4):
                        j = 4 * jb + jj
                        G, Y = pp(), pp()
                        proj(G, wgv, jj * 128, xnT, [wgb, xnT])
                        proj(Y, wbv, jj * 128, branches[n], [wbb, branches[n]])
                        B1 = Bt[(j + n) % 5]
                        sigmoid_from(G[:, :], B1, [G])
                        if n == 0:
                            tt(big[:, j, :], B1[:, 0:TS], Y[:, :], ALU.mult, [B1, Y], [big])
                        elif n == 1:
                            tt(B1[:, 0:TS], B1[:, 0:TS], Y[:, :], ALU.mult, [B1, Y], [B1])
                            tt(big[:, j, :], big[:, j, :], B1[:, 0:TS], ALU.add, [big, B1], [big])
                        else:
                            tt(B1[:, 0:TS], B1[:, 0:TS], Y[:, :], ALU.mult, [B1, Y], [B1])
                            tt(cvm[:, j, :], big[:, j, :], B1[:, 0:TS], ALU.add, [big, B1], [cvm])
            for jb in range(2):
                wb = wget((l, "wo", jb))
                wv = w3(wb, 512)
                for jj in range(4):
                    j = 4 * jb + jj
                    Dp = pp()
                    proj(Dp, wv, jj * 128, cvm, [wb, cvm])
                    tt(hT[:, j, :], hT[:, j, :], Dp[:, :], ALU.add, [hT, Dp], [hT])

        def peer(st, l):
            norm(l, 1, False)
            i1T, i2T, gT = qtg[0], qtg[1], Bt[0]
            for b in range(4):
                wb = wget((l, "pq", b))
                wv = w3(wb, 512)
                for gg in range(4):
                    g = 4 * b + gg
                    pk = g % 2
                    Q = pp()
                    proj(Q, wv, gg * 128, xnT, [wb, xnT])
                    qt = qtg[g % 2]
                    cpa(qt[:, :], Q[:, :], [Q], [qt])
                    for tb in range(4):
                        scp = pp()
                        mm(scp[:, 0:128], qt[:, tb * 128:(tb + 1) * 128], keysT[:, l * 2 + pk, :], True, True, [qt, keysT], [scp])
                        sc = scs[(g * 4 + tb) % 2]
                        cpa(sc[:, :], scp[:, 0:128], [scp], [sc])
                        S.emit("dve", lambda e, sc=sc, tb=tb, g=g: e.max(out=tv[:, tb, g, 0:8], in_=sc[:, :]), reads=[sc], writes=[tv])
                        S.emit("dve", lambda e, sc=sc, tb=tb, g=g: e.max_index(out=ti[:, tb, g, 0:8], in_max=tv[:, tb, g, 0:8], in_values=sc[:, :]),
                               reads=[sc, tv], writes=[ti])
                        S.emit("dve", lambda e, sc=sc, tb=tb, g=g: e.match_replace(out=sc2[:, :], in_to_replace=tv[:, tb, g, 0:8], in_values=sc[:, :],
                                                                                imm_value=-1e30), reads=[sc, tv], writes=[sc2])
                        S.emit("dve", lambda e, tb=tb, g=g: e.max(out=tv[:, tb, g, 8:16], in_=sc2[:, :]), reads=[sc2], writes=[tv])
                        S.emit("dve", lambda e, tb=tb, g=g: e.max_index(out=ti[:, tb, g, 8:16], in_max=tv[:, tb, g, 8:16], in_values=sc2[:, :]),
                               reads=[sc2, tv], writes=[ti])
            for tb in range(4):
                cp("dve", tif[:, :, :], ti[:, tb, :, :], [ti], [tif])
                for h in range(8):
                    in0 = tv[:, tb, 2 * h, :].unsqueeze(2).to_broadcast([128, 16, 16])
                    in1 = tv[:, tb, 2 * h + 1, :].unsqueeze(1).to_broadcast([128, 16, 16])
                    tt(cand[:, :].rearrange("p (a b) -> p a b", a=16), in0, in1, ALU.add, [tv], [cand])
                    S.emit("dve", lambda e, h=h: e.max(out=bs[:, h, 0:8], in_=cand[:, :]), reads=[cand], writes=[bs])
                    S.emit("dve", lambda e, h=h: e.max_index(out=bc[:, h, 0:8], in_max=bs[:, h, 0:8], in_values=cand[:, :]),
                           reads=[cand, bs], writes=[bc])
                    S.emit("dve", lambda e, h=h: e.match_replace(out=cand2[:, :], in_to_replace=bs[:, h, 0:8], in_values=cand[:, :],
                                                                 imm_value=-1e30), reads=[cand, bs], writes=[cand2])
                    S.emit("dve", lambda e, h=h: e.max(out=bs[:, h, 8:16], in_=cand2[:, :]), reads=[cand2], writes=[bs])
                    S.emit("dve", lambda e, h=h: e.max_index(out=bc[:, h, 8:16], in_max=bs[:, h, 8:16], in_values=cand2[:, :]),
                           reads=[cand2, bs], writes=[bc])
                bcf = bc[:, :, :].rearrange("p h k -> p (h k)")
                S.emit("dve", lambda e, bcf=bcf: e.tensor_single_scalar(out=au[:, :], in_=bcf, scalar=4, op=ALU.arith_shift_right),
                       reads=[bc], writes=[au])
                S.emit("dve", lambda e, bcf=bcf: e.tensor_single_scalar(out=bu[:, :], in_=bcf, scalar=15, op=ALU.bitwise_and),
                       reads=[bc], writes=[bu])
                cp("dve", af[:, :, :].rearrange("p h k -> p (h k)"), au[:, :], [au], [af])
                cp("dve", bf[:, :, :].rearrange("p h k -> p (h k)"), bu[:, :], [bu], [bf])
                for h in range(8):
                    for (sel, src_g, dst) in ((af, 2 * h, i1), (bf, 2 * h + 1, i2)):
                        eq3 = eqb[:, :].rearrange("p (k a) -> p k a", k=16)
                        tt(eq3, io16[:, :].unsqueeze(1).to_broadcast([128, 16, 16]),
                           sel[:, h, :].unsqueeze(2).to_broadcast([128, 16, 16]), ALU.is_equal, [io16, sel], [eqb])
                        tt(eq3, eq3, tif[:, src_g, :].unsqueeze(1).to_broadcast([128, 16, 16]), ALU.mult, [eqb, tif], [eqb])
                        red(dst[:, h, :], eq3, ALU.add, [eqb], [dst])
                tt(gw[:, :, :], bs[:, :, :], bs[:, :, 0:1].to_broadcast([128, 8, 16]), ALU.subtract, [bs], [gw])
                act(gw[:, :, :], gw[:, :, :], AF.Exp, [gw], [gw])
                red(gs8[:, :], gw[:, :, :], ALU.add, [gw], [gs8])
                recip(gs8[:, :], gs8[:, :], [gs8], [gs8])
                tt(gw[:, :, :], gw[:, :, :], gs8[:, :].unsqueeze(2).to_broadcast([128, 8, 16]), ALU.mult, [gw, gs8], [gw])
                for (srcb, dstb) in ((i1, i1T), (i2, i2T), (gw, gT)):
                    bank = pp()
                    tr(bank[:, 0:128], srcb[:, :, :].rearrange("p h k -> p (h k)"), ident[:, :], [srcb, ident], [bank])
                    cpa(dstb[:, tb * 128:(tb + 1) * 128], bank[:, 0:128], [bank], [dstb])
            if st == 0 and l == 0:
                dump("i1T", i1T, i1T[:, 0:512], [128, 512])
                dump("i2T", i2T, i2T[:, 0:512], [128, 512])
                dump("gT", gT, gT[:, 0:512], [128, 512])

            def accap(c):
                if c < 4:
                    return pbanks[c // 2], pbanks[c // 2][:, (c % 2) * 256:(c % 2) * 256 + 256]
                return pbig, pbig[:, (c - 4) * 256:(c - 4) * 256 + 256]

            for hf in range(2):
                ppbase[0] = 0
                for sbk in range(64):
                    t0 = hf * 256 + sbk * 4
                    O2, O1 = ohA[sbk % 2], ohB[sbk % 2]
                    iob = io128[:, :].unsqueeze(1).to_broadcast([128, 4, 128])
                    tt(O2[:, :, :], iob, i2T[:, t0:t0 + 4].unsqueeze(2).to_broadcast([128, 4, 128]), ALU.is_equal,
                       [io128, i2T], [O2])
                    tt(O1[:, :, :], iob, i1T[:, t0:t0 + 4].unsqueeze(2).to_broadcast([128, 4, 128]), ALU.is_equal,
                       [io128, i1T], [O1])
                    tt(O1[:, :, :], O1[:, :, :], gT[:, t0:t0 + 4].unsqueeze(2).to_broadcast([128, 4, 128]), ALU.mult,
                       [O1, gT], [O1], eng="pool")
                    bank = pp()
                    for tq in range(4):
                        mm(bank[:, tq * 128:(tq + 1) * 128], O2[:, tq, :], O1[:, tq, :], True, True, [O2, O1], [bank])
                    tl = sbk * 4
                    outv = S3[:, tl:tl + 4, :]
                    inv = bank[:, :].rearrange("p (t i) -> p t i", t=4)
                    cpa(outv, inv, [bank], SALL)
                ppbase[0] = 2
                LAG = 2
                tinfo = {}

                def emitH(i1v):
                    wb = wget((l, "ud", i1v))
                    UTv = wb[:, 0:1024].rearrange("p (k n) -> p k n", k=8)
                    Vv = wb[:, 1024:2048]
                    Hb = pp()
                    for kc in range(8):
                        mm(Hb[:, 0:256], UTv[:, kc, :], xnT[:, kc, hf * 256:(hf + 1) * 256], kc == 0, kc == 7, [wb, xnT], [Hb])
                    geB, GtB = (Bt[1], xcb)[i1v % 2], Bt[2 + i1v % 3]
                    gev = geB[:, 0:128].bitcast(BF16) if i1v % 2 == 0 else xcb[:, 0, 0:256]
                    Gtv = GtB[:, 0:128].bitcast(BF16)
                    act(gev, Hb[:, 0:256], AF.Gelu_apprx_tanh, [Hb], [geB])
                    tt(Gtv, gev, S3[:, :, i1v], ALU.mult, [geB] + SALL, [GtB])
                    tinfo[i1v] = (wb, Vv, GtB, Gtv)

                def emitO(i1v):
                    wb, Vv, GtB, Gtv = tinfo.pop(i1v)
                    for c in range(8):
                        ab, aap = accap(c)
                        mm(aap, Vv[:, c * 128:(c + 1) * 128], Gtv, i1v == 0 and c % 2 == 0, i1v == 127, [wb, GtB], [ab])

                for i in range(128 + LAG):
                    if i < 128:
                        emitH(i)
                    if i >= LAG:
                        emitO(i - LAG)
                for c in range(8):
                    ab, aap = accap(c)
                    tt(hT[:, c, hf * 256:(hf + 1) * 256], hT[:, c, hf * 256:(hf + 1) * 256], aap, ALU.add, [hT, ab], [hT])
                ppbase[0] = 0

        def ple(st, l):
            norm(l, 2, False)
            ptok = acc
            ptv_ = acc[:, :].rearrange("p (b d) -> p b d", b=4)
            dma("sp", ptv_, p_d[l, st * TS:(st + 1) * TS, :].rearrange("(b p) d -> p b d", p=128), dma_out=ptok)
            for kc in range(2):
                bank = pp()
                for tb in range(4):
                    tr(bank[:, tb * 128:(tb + 1) * 128], ptv_[:, tb, kc * 128:(kc + 1) * 128], ident[:, :], [ptok, ident], [bank])
                cpa(pT[:, kc, :], bank[:, :], [bank], [pT])
            for jb in range(2):
                wgb = wget((l, "pg", jb))
                wpb = wget((l, "pp", jb))
                wgv, wpv = w3(wgb, 512), w3(wpb, 512, kc=2)
                for jj in range(4):
                    j = 4 * jb + jj
                    G, Pj = pp(), pp()
                    proj(G, wgv, jj * 128, xnT, [wgb, xnT])
                    proj(Pj, wpv, jj * 128, pT, [wpb, pT], kcs=2)
                    B1 = Bt[j % 5]
                    sigmoid_from(G[:, :], B1, [G])
                    tt(B1[:, 0:TS], B1[:, 0:TS], Pj[:, :], ALU.mult, [B1, Pj], [B1])
                    tt(hT[:, j, :], hT[:, j, :], B1[:, 0:TS], ALU.add, [hT, B1], [hT])

        def final(st):
            norm(L, 0, True)
            for tb in range(4):
                for c in range(8):
                    tr(pbig[:, c * 128:(c + 1) * 128], big[:, c, tb * 128:(tb + 1) * 128], ident[:, :], [big, ident], [pbig])
                cpa(xtok[:, :], pbig[:, :], [pbig], [xtok])
                r0 = st * TS + tb * 128
                dma("sp", o_d[r0:r0 + 128, :], xtok[:, :], dma_in=xtok)

        def dump_h(name):
            dump(name, hT, hT[:, :, :], [128, 8, TS])

        for st in range(NST):
            load_x(st)
            if st == 0:
                dump_h("h_in")
            for l in range(L):
                mixer(st, l)
                if st == 0:
                    ut_prep(l)
                if st == 0 and l == 0:
                    dump("xn", xnT, xnT[:, :, :], [128, 8, TS], BF16)
                    dump("attn", attnT, attnT[:, :, :], [128, 8, TS], BF16)
                    dump("rec", recT, recT[:, :, :], [128, 8, TS], BF16)
                    dump("mrg", cvm, cvm[:, :, :], [128, 8, TS], BF16)
                    dump_h("h_mix")
                peer(st, l)
                if st == 0 and l == 0:
                    dump_h("h_peer")
                ple(st, l)
                if st == 0 and l == 0:
                    dump_h("h_ple")
            final(st)
        S.wait_all("sp", [xtok, hT, big, cvm, attnT, recT, xnT, qtg[0], qtg[1], Bt[0], Bt[1], Bt[2], Bt[3], Bt[4], xr, xc, dcol])
        return wrec

    worder = prog(DrySched(), None)
    pidx[0] = 0
    ppbase[0] = 0
    Srec = Sched(nc, es, needed=None)
    prog(Srec, worder)
    for b_ in Buf.ALL:
        b_.reset()
    pidx[0] = 0
    ppbase[0] = 0
    S = Sched(nc, es, needed=Srec.targets)
    prog(S, worder)
    S.flush()
    es.close()
    return nc, S, dbg_outs


def prep_shared(inp):
    inp = {k: np.asarray(v) for k, v in inp.items()}
    sh = {}
    sh["wts"] = host_weights(inp)
    sh["cols"] = host_cols(inp)
    sh["sinks"] = np.ascontiguousarray(np.broadcast_to(inp["attn_sinks"].reshape(1, L * 16), (128, L * 16))).astype(np.float32)
    kT = inp["peer_sub_keys"].transpose(3, 0, 1, 2)
    sh["keysT"] = np.ascontiguousarray(kT).reshape(128, L * 2 * 128).astype(np.float32)
    sh["pu"] = np.ascontiguousarray(inp["peer_u"].reshape(L * NEXP, D))
    sh["pv"] = np.ascontiguousarray(inp["peer_v"].reshape(L * NEXP, D))
    return sh


def kernel(**inputs):
    x = np.asarray(inputs["x"])
    p = np.asarray(inputs["p"])
    B, S_LEN, _ = x.shape
    sh = prep_shared(inputs)
    nc, _, _ = build_nc(S_LEN)
    in_maps = []
    for b in range(B):
        m = dict(sh)
        m["x"] = np.ascontiguousarray(x[b])
        m["p"] = np.ascontiguousarray(p[:, b])
        in_maps.append(m)
    res = run_bass_kernel_spmd(nc, in_maps, core_ids=list(range(B)))
    return np.stack([res.results[b]["out"] for b in range(B)], 0).astype(np.float32)
```

```python
import numpy as np
from contextlib import ExitStack
import concourse.bass as bass
import concourse.mybir as mybir
from concourse.bass_utils import run_bass_kernel_spmd

F32 = mybir.dt.float32
BF16 = mybir.dt.bfloat16
I32 = mybir.dt.int32
U32 = mybir.dt.uint32
AF = mybir.ActivationFunctionType
ALU = mybir.AluOpType
AX = mybir.AxisListType

D = 1024
L = 2
TS = 512
NEXP = 16384
NPL = 14
NWB = 3
NGB = 6
GC = 1.5957691216057308


class Buf:
    ALL = []

    def __init__(self, name, t):
        self.name = name
        self.t = t
        self.reset()
        Buf.ALL.append(self)

    def reset(self):
        self.last_w = None
        self.readers = {}
        self.dsem = None
        self.dcnt = 0
        self.dep = 0

    def __getitem__(self, k):
        return self.t[k]


class Sched:
    ENGS = ("pe", "act", "dve", "pool", "sp")
    EPOCH = 30000
    DEPOCH = 24000

    def __init__(self, nc, es, needed=None):
        self.nc = nc
        self.es = es
        self.record = needed is None
        self.needed = needed
        self.targets = {e: set() for e in self.ENGS}
        self.incidx = {e: {} for e in self.ENGS}
        self.ninc = {e: 0 for e in self.ENGS}
        self.q = {e: [] for e in self.ENGS}
        self.cnt = {e: 0 for e in self.ENGS}
        self.sems = {e: [] for e in self.ENGS}
        self.seen = {e: {} for e in self.ENGS}
        self.dma_sems = {}
        self.nsem = 0
        self.ninst = 0

    def _newsem(self, name):
        self.nsem += 1
        if self.record:
            return ("dummy", name)
        return self.es.enter_context(self.nc.semaphore(name))

    def _esem(self, e, k):
        ep = (k - 1) // self.EPOCH
        while len(self.sems[e]) <= ep:
            self.sems[e].append(self._newsem(f"s_{e}_{len(self.sems[e])}"))
        return self.sems[e][ep], k - ep * self.EPOCH

    def _res(self, key, val):
        if isinstance(key, tuple):
            return (self.dma_sems[key], val)
        if self.record:
            self.targets[key].add(val)
            return (None, val)
        return self._esem(key, self.incidx[key][val])

    def emit(self, e, fn, reads=(), writes=(), dma_out=None, dma_in=None, dma_out2=(), chain=False):
        def flat(lst):
            out = []
            for x in lst:
                if hasattr(x, "bufs"):
                    out.extend(x.bufs)
                else:
                    out.append(x)
            return out
        reads, writes = flat(reads), flat(writes)
        waits = {}

        def need(dep):
            if dep is None:
                return
            key, val = dep
            if waits.get(key, 0) < val:
                waits[key] = val

        for b in reads:
            need(b.last_w)
        for b in writes:
            need(b.last_w)
            for d in b.readers.values():
                need(d)
        if dma_out is not None and not chain:
            need(dma_out.last_w)
            for d in dma_out.readers.values():
                need(d)
        if dma_in is not None:
            need(dma_in.last_w)
        for b2 in dma_out2:
            need(b2.last_w)
            for d in b2.readers.values():
                need(d)
        wl = []
        for key, val in waits.items():
            if key == e and e == "pe":
                continue
            if self.seen[e].get(key, 0) >= val:
                continue
            self.seen[e][key] = val
            wl.append((key, val))
        self.ninst += 1
        is_dma = dma_out is not None or dma_in is not None
        if is_dma:
            b = dma_out if dma_out is not None else dma_in
            if b.dsem is None or b.dcnt + 16 > self.DEPOCH:
                b.dsem = self._newsem(f"d_{b.name}_{b.dep}")
                b.dep += 1
                b.dcnt = 0
            b.dcnt += 16
            dkey = ("dma", id(b), b.dep)
            self.dma_sems[dkey] = b.dsem
            dep_me = (dkey, b.dcnt)
            inc = (b.dsem, 16)
            if dma_out is not None:
                b.last_w = dep_me
                b.readers = {}
                for b2 in dma_out2:
                    b2.last_w = dep_me
                    b2.readers = {}
            else:
                b.readers[dkey] = dep_me
            for r in reads:
                r.readers[dkey] = dep_me
        else:
            self.cnt[e] += 1
            n = self.cnt[e]
            dep_me = (e, n)
            inc = None
            if (not self.record) and n in self.needed[e]:
                self.ninc[e] += 1
                self.incidx[e][n] = self.ninc[e]
                mysem, _ = self._esem(e, self.ninc[e])
                inc = (mysem, 1)
            for r in reads:
                r.readers[e] = dep_me
            for w in writes:
                w.last_w = dep_me
                w.readers = {}
        wres = [self._res(k, v) for k, v in wl]
        if self.record:
            return

        def run(engine):
            for s, v in wres:
                engine.wait_ge(s, v)
            ins = fn(engine)
            if inc is not None:
                ins.then_inc(inc[0], inc[1])
        self.q[e].append(run)

    def wait_all(self, e, bufs):
        wl = []
        for b in bufs:
            if b.last_w is not None:
                wl.append(b.last_w)
            wl.extend(b.readers.values())
        wres = [self._res(k, v) for k, v in wl]
        if self.record:
            return

        def run(engine):
            for s, v in wres:
                engine.wait_ge(s, v)
        self.q[e].append(run)

    def flush(self):
        q = self.q
        with self.nc.Block() as block:
            @block.tensor
            def _(eng):
                for f in q["pe"]:
                    f(eng)

            @block.scalar
            def _(eng):
                for f in q["act"]:
                    f(eng)

            @block.vector
            def _(eng):
                for f in q["dve"]:
                    f(eng)

            @block.gpsimd
            def _(eng):
                for f in q["pool"]:
                    f(eng)

            @block.sync
            def _(eng):
                for f in q["sp"]:
                    f(eng)


class DrySched:
    def emit(self, *a, **k):
        pass

    def wait_all(self, *a, **k):
        pass


def wlayout():
    out = []
    out.append((("kk",), 4096, 128))
    out.append((("vv",), 2048, 128))
    for b in range(2):
        out.append((("q", b), 4096, 128))
    for j in range(8):
        out.append((("cv", j), 3072, 128))
    for jh in range(4):
        out.append((("rc", jh), 4096, 128))
        out.append((("rg", jh), 1024, 128))
    for n in range(3):
        for jb in range(2):
            out.append((("gt", n, jb), 4096, 128))
            out.append((("br", n, jb), 4096, 128))
    for jb in range(2):
        out.append((("wo", jb), 4096, 128))
    for b in range(4):
        out.append((("pq", b), 4096, 128))
    for jb in range(2):
        out.append((("pg", jb), 4096, 128))
        out.append((("pp", jb), 1024, 128))
    return out


def woffsets():
    offs = {}
    o = 0
    for l in range(L):
        for key, F, parts in wlayout():
            offs[(l,) + key] = (o, F, parts)
            o += F
    return offs, o


def _kblk(w):
    K, N = w.shape
    return np.ascontiguousarray(w.reshape(K // 128, 128, N).transpose(1, 0, 2)).reshape(128, -1)


def host_weights(inp):
    offs, tot = woffsets()
    W = np.zeros((128, tot), np.float32)

    def put(key, arr):
        o, F, parts = offs[key]
        assert arr.shape == (128, F), (key, arr.shape, F)
        W[:, o:o + F] = arr

    for l in range(L):
        wi = inp["w_in"][l]
        k = wi[:, 1024:1280].reshape(D, 4, 1, 64)
        put((l, "kk"), _kblk(np.broadcast_to(k, (D, 4, 2, 64)).reshape(D, 512)))
        put((l, "vv"), _kblk(wi[:, 1280:1536]))
        for b in range(2):
            put((l, "q", b), _kblk(wi[:, 512 * b:512 * b + 512]))
        for j in range(8):
            cols = np.concatenate([wi[:, 1536 + 128 * j:1536 + 128 * j + 128],
                                   wi[:, 2560 + 128 * j:2560 + 128 * j + 128],
                                   wi[:, 3584 + 128 * j:3584 + 128 * j + 128]], axis=1)
            put((l, "cv", j), _kblk(cols))
        for jh in range(4):
            cols = np.concatenate([wi[:, 4608 + 256 * jh:4608 + 256 * jh + 256],
                                   wi[:, 5632 + 256 * jh:5632 + 256 * jh + 256]], axis=1)
            put((l, "rc", jh), _kblk(cols))
            g = np.stack([_kblk(inp["w_rgate"][l, jh]).reshape(128, 2, 256),
                          _kblk(inp["w_igate"][l, jh]).reshape(128, 2, 256)], axis=1)
            put((l, "rg", jh), g.reshape(128, 1024))
        for n in range(3):
            for jb in range(2):
                c0 = 6656 + n * 1024 + 512 * jb
                put((l, "gt", n, jb), _kblk(wi[:, c0:c0 + 512]))
                put((l, "br", n, jb), _kblk(inp["w_branch"][l, n][:, 512 * jb:512 * jb + 512]))
        for jb in range(2):
            put((l, "wo", jb), _kblk(inp["w_out"][l][:, 512 * jb:512 * jb + 512]))
        for b in range(4):
            put((l, "pq", b), _kblk(inp["w_peer_q"][l][:, 512 * b:512 * b + 512]))
        for jb in range(2):
            put((l, "pg", jb), _kblk(inp["w_ple_gate"][l][:, 512 * jb:512 * jb + 512]))
            put((l, "pp", jb), _kblk(inp["w_ple_proj"][l][:, 512 * jb:512 * jb + 512]))
    return W


def host_cols(inp):
    vecs = []
    for l in range(L):
        vecs += [inp["norm_mix"][l], inp["norm_ffn"][l], inp["norm_ple"][l],
                 inp["conv_w"][l, 0], inp["conv_w"][l, 1], inp["conv_w"][l, 2],
                 inp["rec_conv_w"][l, 0], inp["rec_conv_w"][l, 1], inp["rec_conv_w"][l, 2],
                 inp["rec_conv_w"][l, 3], inp["rec_conv_b"][l], inp["b_rgate"][l],
                 inp["b_igate"][l], inp["lru_lambda"][l]]
    vecs.append(inp["norm_final"])
    a = np.stack([np.asarray(v, np.float32) for v in vecs], 0)
    a = a.reshape(a.shape[0], 8, 128).transpose(2, 0, 1)
    return np.ascontiguousarray(a).reshape(128, -1)


def build_nc(S_LEN, dbg=False):
    NST = S_LEN // TS
    nc = bass.Bass("TRN2", target_bir_lowering=False)
    es = ExitStack()
    Buf.ALL = []
    offs, WTOT = woffsets()
    NCOL = (L * NPL + 1) * 8

    x_d = nc.dram_tensor("x", [S_LEN, D], F32, kind="ExternalInput").ap()
    p_d = nc.dram_tensor("p", [L, S_LEN, 256], F32, kind="ExternalInput").ap()
    w_d = nc.dram_tensor("wts", [128, WTOT], F32, kind="ExternalInput").ap()
    c_d = nc.dram_tensor("cols", [128, NCOL], F32, kind="ExternalInput").ap()
    s_d = nc.dram_tensor("sinks", [128, L * 16], F32, kind="ExternalInput").ap()
    k_d = nc.dram_tensor("keysT", [128, L * 2 * 128], F32, kind="ExternalInput").ap()
    pu_d = nc.dram_tensor("pu", [L * NEXP, D], F32, kind="ExternalInput").ap()
    pv_d = nc.dram_tensor("pv", [L * NEXP, D], F32, kind="ExternalInput").ap()
    o_d = nc.dram_tensor("out", [S_LEN, D], F32, kind="ExternalOutput").ap()
    pvb_d = nc.dram_tensor("pvb_scr", [L * NEXP, D], BF16).ap()
    ut_d = nc.dram_tensor("ut_scr", [L * 128 * 128, D], BF16).ap()
    ut_bl = [Buf(f"ut_b{l}", None) for l in range(L)]
    pub_b = Buf("pub_b", None)
    pvb_bl = [Buf(f"pvb_b{l}", None) for l in range(L)]
    dbg_outs = {}

    cnt = [0]

    def sb(shape, dt, name=None):
        cnt[0] += 1
        name = "sb_" + (name or f"t{cnt[0]}")
        return Buf(name, es.enter_context(nc.sbuf_tensor(name, shape, dt)))

    def psb(shape, dt, name):
        return Buf(name, es.enter_context(nc.psum_tensor(name, shape, dt)))

    hT = sb([128, 8, TS], F32, "hT")
    xnT = sb([128, 8, TS], BF16, "xnT")
    arena_t = es.enter_context(nc.sbuf_tensor("sb_arena", [128, 32768], BF16))
    big = Buf("big", arena_t[:, 0:8192].bitcast(F32).rearrange("p (c t) -> p c t", c=8))
    attnT = Buf("attnT", arena_t[:, 8192:12288].rearrange("p (c t) -> p c t", c=8))
    cvm = Buf("cvm", arena_t[:, 12288:16384].rearrange("p (c t) -> p c t", c=8))
    recT = Buf("recT", arena_t[:, 16384:20480].rearrange("p (c t) -> p c t", c=8))
    sx = Buf("sx", arena_t[:, 20480:32768])
    SALL = [big, attnT, cvm, recT, sx]
    S3 = arena_t[:, :].rearrange("p (t i) -> p t i", i=128)
    KT = [sb([128, 4, 128 + TS], BF16, f"KT{l}") for l in range(L)]
    VX = [sb([128, 5, 4, 192], BF16, f"VX{l}") for l in range(L)]
    wb_t = es.enter_context(nc.sbuf_tensor("sb_wball", [128, 6 * 2048], BF16))
    whalf = [Buf(f"wh{i}", wb_t[:, i * 2048:(i + 1) * 2048]) for i in range(6)]

    class WView:
        def __init__(self, s0, n):
            self.bufs = [whalf[s0 + k] for k in range(n)]
            self.ap = wb_t[:, s0 * 2048:(s0 + n) * 2048]

        def __getitem__(self, k):
            return self.ap[k]
    rstd = sb([128, TS], F32, "rstd")
    Bt = [sb([128, 520], F32, f"B{i}") for i in range(5)]
    xr = sb([128, 2, 520], F32, "xr")
    xc = sb([128, 2, TS], F32, "xc")
    xcb = sb([128, 2, TS], BF16, "xcb")
    qh2 = [sb([128, TS], BF16, f"qh{i}") for i in range(2)]
    Pb2 = [sb([128, 256], BF16, f"Pb{i}") for i in range(2)]
    pTs2 = [sb([128, 256], BF16, f"pTs{i}") for i in range(2)]
    sm2 = [[sb([128, 1], F32, f"sm{k}_{i}") for i in range(6)] for k in range(2)]
    conv_tail = sb([128, L, 8, 2], F32, "ctail")
    rec_tail = sb([128, L, 8, 3], F32, "rtail")
    lru = sb([128, L, 8], F32, "lru")
    tv = sb([128, 4, 16, 16], F32, "tv")
    ti = sb([128, 4, 16, 16], U32, "ti")
    tif = sb([128, 16, 16], F32, "tif")
    scs = [sb([128, 128], F32, f"scs{i}") for i in range(2)]
    sc2 = sb([128, 128], F32, "sc2")
    qtg = [sb([128, TS], F32, f"qtg{i}") for i in range(2)]
    cand = sb([128, 256], F32, "cand")
    cand2 = sb([128, 256], F32, "cand2")
    eqb = sb([128, 256], F32, "eqb")
    bs = sb([128, 8, 16], F32, "bs")
    bc = sb([128, 8, 16], U32, "bc")
    au = sb([128, 128], U32, "au")
    bu = sb([128, 128], U32, "bu")
    af = sb([128, 8, 16], F32, "af")
    bf = sb([128, 8, 16], F32, "bf")
    i1 = sb([128, 8, 16], F32, "i1")
    i2 = sb([128, 8, 16], F32, "i2")
    gw = sb([128, 8, 16], F32, "gw")
    gs8 = sb([128, 8], F32, "gs8")
    wgt = sb([128, 2], F32, "wgt")
    xtok = sb([128, D], F32, "xtok")
    acc = sb([128, D], F32, "acc")
    ohA = [sb([128, 4, 128], BF16, f"ohA{i}") for i in range(2)]
    ohB = [sb([128, 4, 128], BF16, f"ohB{i}") for i in range(2)]
    io128 = sb([128, 128], F32, "io128")
    eidx4 = wgt
    pT = sb([128, 2, TS], BF16, "pT")
    ident = sb([128, 128], F32, "ident")
    identb = sb([128, 128], BF16, "identb")
    onesm = sb([128, 128], F32, "onesm")
    mask = sb([128, 256], F32, "mask")
    mask0 = sb([128, 256], F32, "mask0")
    iov = cand
    io16 = sb([128, 16], F32, "io16")
    cols = sb([128, NCOL], F32, "cols")
    dcol = sb([128, L, 3, 8], F32, "dcol")
    sinks = sb([128, L * 16], F32, "sinks")
    keysT = sb([128, L * 2, 128], F32, "keysT")
    epsc = sb([128, 1], F32, "epsc")
    pbanks = [psb([128, 512], F32, f"pb{i}") for i in range(6)]
    pbig = psb([128, 1024], F32, "pbig")
    pidx = [0]
    ppbase = [0]

    def pp():
        n = 6 - ppbase[0]
        b = pbanks[ppbase[0] + pidx[0] % n]
        pidx[0] += 1
        return b

    def prog(S, worder):
        def mm(out, lhsT, rhs, start, stop, R, Wr):
            S.emit("pe", lambda e: e.matmul(out, lhsT=lhsT, rhs=rhs, start=start, stop=stop), reads=R, writes=Wr)

        def tr(out, in_, idn, R, Wr):
            S.emit("pe", lambda e: e.transpose(out, in_, idn), reads=R, writes=Wr)

        def act(out, in_, func, R, Wr, bias=0.0, scale=1.0, accum=None):
            S.emit("act", lambda e: e.activation(out=out, in_=in_, func=func, bias=bias, scale=scale, accum_out=accum),
                   reads=R, writes=Wr)

        def cpa(out, in_, R, Wr):
            S.emit("act", lambda e: e.copy(out=out, in_=in_), reads=R, writes=Wr)

        def cp(eng, out, in_, R, Wr):
            S.emit(eng, lambda e: e.tensor_copy(out=out, in_=in_), reads=R, writes=Wr)

        def tt(out, in0, in1, op, R, Wr, eng="dve"):
            S.emit(eng, lambda e: e.tensor_tensor(out=out, in0=in0, in1=in1, op=op), reads=R, writes=Wr)

        def ts(out, in0, s1, s2, op0, op1, R, Wr, eng="dve"):
            if s2 is None:
                S.emit(eng, lambda e: e.tensor_scalar(out=out, in0=in0, scalar1=s1, scalar2=None, op0=op0), reads=R, writes=Wr)
            else:
                S.emit(eng, lambda e: e.tensor_scalar(out=out, in0=in0, scalar1=s1, scalar2=s2, op0=op0, op1=op1),
                       reads=R, writes=Wr)

        def stt(out, in0, scalar, in1, op0, op1, R, Wr, accum=None):
            S.emit("dve", lambda e: e.scalar_tensor_tensor(out=out, in0=in0, scalar=scalar, in1=in1, op0=op0, op1=op1,
                                                           accum_out=accum), reads=R, writes=Wr)

        def recip(out, in_, R, Wr):
            S.emit("dve", lambda e: e.reciprocal(out=out, in_=in_), reads=R, writes=Wr)

        def red(out, in_, op, R, Wr):
            S.emit("dve", lambda e: e.tensor_reduce(out=out, in_=in_, axis=AX.X, op=op), reads=R, writes=Wr)

        def memset(eng, ap, val, Wr):
            S.emit(eng, lambda e: e.memset(ap, val), writes=Wr)

        def dma(eng, out, in_, dma_out=None, dma_in=None, R=(), dma_out2=(), chain=False):
            S.emit(eng, lambda e: e.dma_start(out=out, in_=in_), reads=R, dma_out=dma_out, dma_in=dma_in, dma_out2=dma_out2,
                   chain=chain)

        def gather(outb, table, idx_ap, tabbuf):
            S.emit("pool", lambda e: e.indirect_dma_start(out=outb[:, :], out_offset=None, in_=table,
                                                          in_offset=bass.IndirectOffsetOnAxis(ap=idx_ap, axis=0)),
                   reads=[eidx4, tabbuf], dma_out=outb)

        def dump(name, buf, ap, shape, dt=F32):
            if not dbg:
                return
            if name not in dbg_outs:
                dbg_outs[name] = nc.dram_tensor("dbg_" + name, list(shape), dt, kind="ExternalOutput").ap()
            dma("sp", dbg_outs[name], ap, dma_in=buf)

        wstate = {"i": 0, "issued": 0}
        wrec = []
        wslots = []
        if worder is not None:
            p_ = 0
            for key in worder:
                if key[1] == "ud":
                    wslots.append((p_ % 6, 1))
                    p_ += 1
                else:
                    if p_ % 2:
                        p_ += 1
                    wslots.append((p_ % 6, 2))
                    p_ += 2

        def wdead(m, i):
            return (m <= i - 3) if worder[m][1] == "ud" else (m <= i - 2)

        def wview(i):
            s0, n = wslots[i]
            return WView(s0, n)

        def wissue(i):
            key = worder[i]
            wv_ = wview(i)
            if key[1] == "ud":
                l_, i1_ = key[0], key[2]
                r0 = (l_ * 128 + i1_) * 128
                dma("sp", wv_[:, 0:1024], ut_d[r0:r0 + 128, :], dma_out=wv_.bufs[0], R=[ut_bl[l_]])
                r1 = l_ * NEXP + i1_ * 128
                dma("sp", wv_[:, 1024:2048], pvb_d[r1:r1 + 128, :], dma_out=wv_.bufs[0], R=[pvb_bl[l_]], chain=True)
                return
            o, F, parts = offs[key]
            dma("pool", wv_[:, 0:F], w_d[:, o:o + F], dma_out=wv_.bufs[0], dma_out2=wv_.bufs[1:])

        def wget(key):
            if worder is None:
                wrec.append(key)
                return WView(0, 2)
            i = wstate["i"]
            assert worder[i] == key, (worder[i], key)
            while wstate["issued"] < len(worder):
                k = wstate["issued"]
                if k > i + 3:
                    break
                s0, n = wslots[k]
                mine = set((s0 + q) % 6 for q in range(n))
                ok = True
                for m in range(max(0, k - 6), k):
                    if wdead(m, i):
                        continue
                    sm0, nm = wslots[m]
                    if mine & set((sm0 + q) % 6 for q in range(nm)):
                        ok = False
                        break
                if not ok:
                    if k <= i:
                        raise RuntimeError("weight ring deadlock")
                    break
                wissue(k)
                wstate["issued"] += 1
            wstate["i"] += 1
            return wview(i)

        def w3(wb, ncols, kc=8):
            return wb[:, 0:kc * ncols].rearrange("p (k n) -> p k n", k=kc)

        def colp(l, idx, c):
            o = ((l * NPL + idx) * 8 + c) if l < L else (L * NPL * 8 + c)
            return cols[:, o:o + 1]

        dma("sp", cols[:, :], c_d, dma_out=cols)
        dma("sp", sinks[:, :], s_d, dma_out=sinks)
        dma("sp", keysT[:, :, :], k_d.rearrange("p (a n) -> p a n", n=128), dma_out=keysT)
        S.emit("pool", lambda e: e.iota(iov[:, :], [[1, 256]], base=0, channel_multiplier=-1,
                                        allow_small_or_imprecise_dtypes=True), writes=[iov])
        S.emit("pool", lambda e: e.iota(io16[:, :], [[1, 16]], base=0, channel_multiplier=0,
                                        allow_small_or_imprecise_dtypes=True), writes=[io16])
        ts(ident[:, :], iov[:, 0:128], 0.0, None, ALU.is_equal, None, [iov], [ident])
        cp("dve", identb[:, :], ident[:, :], [ident], [identb])
        memset("pool", onesm[:, :], 1.0 / D, [onesm])
        memset("pool", epsc[:, :], 1e-6, [epsc])
        ts(mask[:, :], iov[:, :], 1.0, None, ALU.is_ge, None, [iov], [mask])
        ts(mask0[:, :], iov[:, :], 128.0, None, ALU.is_le, None, [iov], [mask0])
        tt(mask[:, :], mask[:, :], mask0[:, :], ALU.mult, [mask, mask0], [mask])
        ts(mask[:, :], mask[:, :], 30000.0, -30000.0, ALU.mult, ALU.add, [mask], [mask])
        cp("dve", mask0[:, :], mask[:, :], [mask], [mask0])
        memset("dve", mask0[:, 0:128], -30000.0, [mask0])
        for l in range(L):
            memset("pool", KT[l][:, :, :], 0.0, [KT[l]])
            memset("pool", VX[l][:, :, :, :], 0.0, [VX[l]])
        memset("pool", conv_tail[:, :, :, :], 0.0, [conv_tail])
        memset("pool", rec_tail[:, :, :, :], 0.0, [rec_tail])
        memset("pool", lru[:, :, :], 0.0, [lru])
        for l in range(L):
            o11 = (l * NPL + 11) * 8
            ts(dcol[:, l, 0, :], cols[:, o11:o11 + 8], -1.0, None, ALU.mult, None, [cols], [dcol])
            ts(dcol[:, l, 1, :], cols[:, o11 + 8:o11 + 16], -1.0, None, ALU.mult, None, [cols], [dcol])
            act(dcol[:, l, 2, :], cols[:, o11 + 16:o11 + 24], AF.Exp, [cols], [dcol], scale=-1.0)
            ts(dcol[:, l, 2, :], dcol[:, l, 2, :], 1.0, None, ALU.add, None, [dcol], [dcol])
            act(dcol[:, l, 2, :], dcol[:, l, 2, :], AF.Ln, [dcol], [dcol])
            ts(dcol[:, l, 2, :], dcol[:, l, 2, :], -8.0, None, ALU.mult, None, [dcol], [dcol])

        S.emit("pool", lambda e: e.iota(io128[:, :], [[1, 128]], base=0, channel_multiplier=0,
                                        allow_small_or_imprecise_dtypes=True), writes=[io128])
        def ut_prep(lq):
          for t in range(lq * 128, (lq + 1) * 128):
              sbf = (Bt[0], Bt[1], qtg[0], qtg[1])[t % 4]
              dbf = Bt[2 + t % 3]
              srcv = sbf[:, 0:512].bitcast(BF16)
              dstv = dbf[:, 0:512].bitcast(BF16)
              dma("pool", srcv, pu_d[t * 128:(t + 1) * 128, :], dma_out=sbf)
              bank = pp()
              bv = bank[:, 0:512].bitcast(BF16)
              for kc in range(8):
                  tr(bv[:, kc * 128:(kc + 1) * 128], srcv[:, kc * 128:(kc + 1) * 128], identb[:, :], [sbf, identb], [bank])
              if t % 2:
                  cpa(dstv, bv, [bank], [dbf])
              else:
                  cp("dve", dstv, bv, [bank], [dbf])
              dma("sp", ut_d[t * 128:(t + 1) * 128, :], dstv, dma_out=ut_bl[lq], R=[dbf])
              vbf = (xc, xr)[t % 2]
              vv = vbf[:, 0, 0:512].bitcast(BF16)
              dma("pool", vv, pv_d[t * 128:(t + 1) * 128, :], dma_out=vbf)
              dma("sp", pvb_d[t * 128:(t + 1) * 128, :], vv, dma_out=pvb_bl[lq], R=[vbf])

        def proj(bank, wv, col0, rhs_buf, R, kcs=8, M=128):
            for kc in range(kcs):
                mm(bank[0:M, :], wv[:, kc, col0:col0 + M], rhs_buf[:, kc, :], kc == 0, kc == kcs - 1, R, [bank])

        def norm(l, idx, want_f32):
            act(big[:, :, :], hT[:, :, :], AF.Square, [hT], [big])
            bank = pp()
            for c in range(8):
                mm(bank[:, :], onesm[:, :], big[:, c, :], c == 0, c == 7, [onesm, big], [bank])
            act(rstd[:, :], bank[:, :], AF.Ln, [bank, epsc], [rstd], bias=epsc[:, :])
            act(rstd[:, :], rstd[:, :], AF.Exp, [rstd], [rstd], scale=-0.5)
            for c in range(8):
                if want_f32:
                    stt(big[:, c, :], hT[:, c, :], colp(l, idx, c), rstd[:, :], ALU.mult, ALU.mult, [hT, cols, rstd], [big])
                else:
                    stt(xnT[:, c, :], hT[:, c, :], colp(l, idx, c), rstd[:, :], ALU.mult, ALU.mult, [hT, cols, rstd], [xnT])

        def sigmoid_from(bank_ap, Bx, R, bias=0.0):
            act(Bx[:, 0:TS], bank_ap, AF.Sigmoid, R + [Bx], [Bx], bias=bias)

        def load_x(st):
            bigv = big[:, :, :].rearrange("p c t -> p (c t)").rearrange("p (b d) -> p b d", b=4)
            dma("sp", bigv, x_d[st * TS:(st + 1) * TS, :].rearrange("(b p) d -> p b d", p=128), dma_out=big)
            for c in range(8):
                bank = pp()
                for tb in range(4):
                    tr(bank[:, tb * 128:(tb + 1) * 128], bigv[:, tb, c * 128:(c + 1) * 128], ident[:, :], [big, ident], [bank])
                cpa(hT[:, c, :], bank[:, :], [bank], [hT])

        def mixer(st, l):
            norm(l, 0, False)
            wb = wget((l, "kk"))
            wv = w3(wb, 512)
            for kvh in range(4):
                bank = pp()
                proj(bank, wv, kvh * 128, xnT, [wb, xnT])
                cpa(KT[l][:, kvh, 128:128 + TS], bank[:, :], [bank], [KT[l]])
            wb = wget((l, "vv"))
            wv = w3(wb, 256)
            for tb in range(4):
                bank = pp()
                for kc in range(8):
                    mm(bank[:, 0:256], xnT[:, kc, tb * 128:(tb + 1) * 128], wv[:, kc, :], kc == 0, kc == 7, [wb, xnT], [bank])
                cpa(VX[l][:, 1 + tb, :, 64:128], bank[:, 0:256].rearrange("p (h d) -> p h d", h=4), [bank], [VX[l]])
            for b in range(2):
                wb = wget((l, "q", b))
                wv = w3(wb, 512)
                for pr in range(4):
                    pair = b * 4 + pr
                    bank = pp()
                    proj(bank, wv, pr * 128, xnT, [wb, xnT])
                    qh = qh2[pair % 2]
                    cpa(qh[:, :], bank[:, :], [bank], [qh])
                    kvh = (2 * pair) // 4
                    for tb in range(4):
                        avp = pp()

                        def chain(hh, avp=avp, qh=qh, pair=pair, tb=tb, kvh=kvh):
                            h = 2 * pair + hh
                            pb0 = hh * 64
                            scp = pp()
                            mm(scp[:, 0:256], qh[pb0:pb0 + 64, tb * 128:(tb + 1) * 128],
                               KT[l][pb0:pb0 + 64, kvh, tb * 128:tb * 128 + 256], True, True, [qh, KT[l]], [scp])
                            mk = mask0 if (st == 0 and tb == 0) else mask
                            sS, eS = Bt[2 * hh], Bt[2 * hh + 1]
                            Pb, pTs, sm = Pb2[hh], pTs2[hh], sm2[hh]
                            yield
                            stt(sS[:, 0:256], scp[:, 0:256], 0.125, mk[:, :], ALU.mult, ALU.add, [scp, mk], [sS])
                            yield
                            red(sm[0][:, :], sS[:, 0:256], ALU.max, [sS], [sm[0]])
                            yield
                            sk = sinks[:, l * 16 + h:l * 16 + h + 1]
                            ts(sm[1][:, :], sm[0][:, :], sk, -1.0, ALU.max, ALU.mult, [sm[0], sinks], [sm[1]])
                            yield
                            act(eS[:, 0:256], sS[:, 0:256], AF.Exp, [sS, sm[1]], [eS, sm[2]], bias=sm[1][:, :], accum=sm[2][:, :])
                            act(sm[3][:, :], sm[1][:, :], AF.Exp, [sm[1], sinks], [sm[3]], bias=sk)
                            yield
                            tt(sm[4][:, :], sm[2][:, :], sm[3][:, :], ALU.add, [sm[2], sm[3]], [sm[4]])
                            yield
                            recip(sm[5][:, :], sm[4][:, :], [sm[4]], [sm[5]])
                            yield
                            ts(Pb[:, :], eS[:, 0:256], sm[5][:, :], None, ALU.mult, None, [eS, sm[5]], [Pb])
                            yield
                            ptp = pp()
                            ptv = ptp[:, 0:128].bitcast(BF16)
                            tr(ptv[:, 0:128], Pb[:, 0:128], identb[:, :], [Pb, identb], [ptp])
                            tr(ptv[:, 128:256], Pb[:, 128:256], identb[:, :], [Pb, identb], [ptp])
                            yield
                            cpa(pTs[:, :], ptv[:, :], [ptp], [pTs])
                            yield
                            c0 = 64 if hh == 0 else 0
                            for kb in range(2):
                                mm(avp[:, 0:128], VX[l][:, tb + kb, kvh, c0:c0 + 128], pTs[:, kb * 128:(kb + 1) * 128],
                                   hh == 0 and kb == 0, hh == 1 and kb == 1, [VX[l], pTs], [avp])

                        gens = [chain(0), chain(1)]
                        while gens:
                            for g_ in list(gens):
                                try:
                                    next(g_)
                                except StopIteration:
                                    gens.remove(g_)
                        cpa(attnT[:, pair, tb * 128:(tb + 1) * 128], avp[:, 0:128], [avp], [attnT])
            cp("pool", KT[l][:, :, 0:128], KT[l][:, :, TS:TS + 128], [KT[l]], [KT[l]])
            cp("pool", VX[l][:, 0, :, :], VX[l][:, 4, :, :], [VX[l]], [VX[l]])
            for j in range(8):
                wb = wget((l, "cv", j))
                wv = w3(wb, 384)
                pb_, pc_, px_ = pp(), pp(), pp()
                proj(pb_, wv, 0, xnT, [wb, xnT])
                proj(pc_, wv, 128, xnT, [wb, xnT])
                proj(px_, wv, 256, xnT, [wb, xnT])
                T0, PR, AC = Bt[0], Bt[1], Bt[2]
                cpa(T0[:, 0:TS], pc_[:, :], [pc_], [T0])
                cp("dve", PR[:, 0:2], conv_tail[:, l, j, :], [conv_tail], [PR])
                tt(PR[:, 2:2 + TS], T0[:, 0:TS], px_[:, :], ALU.mult, [T0, px_], [PR])
                ts(AC[:, 0:TS], PR[:, 0:TS], colp(l, 3, j), None, ALU.mult, None, [PR, cols], [AC])
                stt(AC[:, 0:TS], PR[:, 1:1 + TS], colp(l, 4, j), AC[:, 0:TS], ALU.mult, ALU.add, [PR, cols, AC], [AC])
                stt(AC[:, 0:TS], PR[:, 2:2 + TS], colp(l, 5, j), AC[:, 0:TS], ALU.mult, ALU.add, [PR, cols, AC], [AC])
                tt(cvm[:, j, :], AC[:, 0:TS], pb_[:, :], ALU.mult, [AC, pb_], [cvm])
                cp("dve", conv_tail[:, l, j, :], PR[:, TS:TS + 2], [PR], [conv_tail])
            if st == 0 and l == 0:
                dump("conv", cvm, cvm[:, :, :], [128, 8, TS], BF16)
            for jh in range(4):
                wb = wget((l, "rc", jh))
                wv = w3(wb, 512)
                wgb = wget((l, "rg", jh))
                wg = wgb[:, 0:1024].rearrange("p (g i n) -> p g i n", g=2, i=2)
                for cc in range(2):
                    ch = 2 * jh + cc
                    bank = pp()
                    proj(bank, wv, cc * 128, xnT, [wb, xnT])
                    cpa(xr[:, cc, 3:3 + TS], bank[:, :], [bank], [xr])
                    cp("dve", xr[:, cc, 0:3], rec_tail[:, l, ch, :], [rec_tail], [xr])
                    ts(xc[:, cc, :], xr[:, cc, 0:TS], colp(l, 6, ch), colp(l, 10, ch), ALU.mult, ALU.add, [xr, cols], [xc])
                    for k in range(1, 4):
                        stt(xc[:, cc, :], xr[:, cc, k:k + TS], colp(l, 6 + k, ch), xc[:, cc, :], ALU.mult, ALU.add,
                            [xr, cols, xc], [xc])
                    cp("dve", rec_tail[:, l, ch, :], xr[:, cc, TS:TS + 3], [xr], [rec_tail])
                    cpa(xcb[:, cc, :], xc[:, cc, :], [xc], [xcb])
                for oc in range(2):
                    ch = 2 * jh + oc
                    pr_, pi_, py_ = pp(), pp(), pp()
                    for ic in range(2):
                        mm(pr_[:, :], wg[:, 0, ic, oc * 128:(oc + 1) * 128], xcb[:, ic, :], ic == 0, ic == 1, [wgb, xcb], [pr_])
                    for ic in range(2):
                        mm(pi_[:, :], wg[:, 1, ic, oc * 128:(oc + 1) * 128], xcb[:, ic, :], ic == 0, ic == 1, [wgb, xcb], [pi_])
                    proj(py_, wv, (2 + oc) * 128, xnT, [wb, xnT])
                    B1, B2, B3, B4, B5 = Bt
                    sigmoid_from(pr_[:, :], B1, [pr_, cols], bias=colp(l, 11, ch))
                    sigmoid_from(pi_[:, :], B2, [pi_, cols], bias=colp(l, 12, ch))
                    act(B3[:, 0:TS], B1[:, 0:TS], AF.Exp, [B1, dcol], [B3], scale=dcol[:, l, 2, ch:ch + 1])
                    if st == 0 and l == 0 and ch == 0:
                        dump("xr", xr, xr[:, 0, :], [128, 520])
                        dump("r", B1, B1[:, 0:TS], [128, TS]); dump("i", B2, B2[:, 0:TS], [128, TS])
                        dump("a", B3, B3[:, 0:TS], [128, TS]); dump("xc", xc, xc[:, 0, :], [128, TS])
                        dump("dcol", dcol, dcol[:, 0, :, :], [128, 3, 8])
                    tt(B4[:, 0:TS], B3[:, 0:TS], B3[:, 0:TS], ALU.mult, [B3], [B4])
                    ts(B4[:, 0:TS], B4[:, 0:TS], -1.0, 1.0, ALU.mult, ALU.add, [B4], [B4])
                    ts(B4[:, 0:TS], B4[:, 0:TS], 1e-20, None, ALU.max, None, [B4], [B4])
                    act(B4[:, 0:TS], B4[:, 0:TS], AF.Ln, [B4], [B4])
                    act(B4[:, 0:TS], B4[:, 0:TS], AF.Exp, [B4], [B4], scale=0.5)
                    tt(B4[:, 0:TS], B4[:, 0:TS], B2[:, 0:TS], ALU.mult, [B4, B2], [B4])
                    tt(B4[:, 0:TS], B4[:, 0:TS], xc[:, oc, :], ALU.mult, [B4, xc], [B4])
                    stt(B4[:, 0:1], B3[:, 0:1], lru[:, l, ch:ch + 1], B4[:, 0:1], ALU.mult, ALU.add, [B3, lru, B4], [B4])
                    S.emit("dve", lambda e, B3=B3, B4=B4, B5=B5: e.tensor_tensor_scan(
                        out=B5[:, 0:TS], data0=B3[:, 0:TS], data1=B4[:, 0:TS], initial=0.0, op0=ALU.mult, op1=ALU.add),
                        reads=[B3, B4], writes=[B5])
                    cp("dve", lru[:, l, ch:ch + 1], B5[:, TS - 1:TS], [B5], [lru])
                    if st == 0 and l == 0 and ch == 0:
                        dump("inp", B4, B4[:, 0:TS], [128, TS]); dump("hs", B5, B5[:, 0:TS], [128, TS])
                    act(B1[:, 0:TS], py_[:, :], AF.Gelu_apprx_tanh, [py_, B1], [B1])
                    tt(recT[:, ch, :], B1[:, 0:TS], B5[:, 0:TS], ALU.mult, [B1, B5], [recT])
            branches = [attnT, cvm, recT]
            for n in range(3):
                for jb in range(2):
                    wgb = wget((l, "gt", n, jb))
                    wbb = wget((l, "br", n, jb))
                    wgv, wbv = w3(wgb, 512), w3(wbb, 512)
                    for jj in range(4):
                        j = 4 * jb + jj
                        G, Y = pp(), pp()
                        proj(G, wgv, jj * 128, xnT, [wgb, xnT])
                        proj(Y, wbv, jj * 128, branches[n], [wbb, branches[n]])
                        B1 = Bt[(j + n) % 5]
                        sigmoid_from(G[:, :], B1, [G])
                        if n == 0:
                            tt(big[:, j, :], B1[:, 0:TS], Y[:, :], ALU.mult, [B1, Y], [big])
                        elif n == 1:
                            tt(B1[:, 0:TS], B1[:, 0:TS], Y[:, :], ALU.mult, [B1, Y], [B1])
                            tt(big[:, j, :], big[:, j, :], B1[:, 0:TS], ALU.add, [big, B1], [big])
                        else:
                            tt(B1[:, 0:TS], B1[:, 0:TS], Y[:, :], ALU.mult, [B1, Y], [B1])
                            tt(cvm[:, j, :], big[:, j, :], B1[:, 0:TS], ALU.add, [big, B1], [cvm])
            for jb in range(2):
                wb = wget((l, "wo", jb))
                wv = w3(wb, 512)
                for jj in range(4):
                    j = 4 * jb + jj
                    Dp = pp()
                    proj(Dp, wv, jj * 128, cvm, [wb, cvm])
                    tt(hT[:, j, :], hT[:, j, :], Dp[:, :], ALU.add, [hT, Dp], [hT])

        def peer(st, l):
            norm(l, 1, False)
            i1T, i2T, gT = qtg[0], qtg[1], Bt[0]
            for b in range(4):
                wb = wget((l, "pq", b))
                wv = w3(wb, 512)
                for gg in range(4):
                    g = 4 * b + gg
                    pk = g % 2
                    Q = pp()
                    proj(Q, wv, gg * 128, xnT, [wb, xnT])
                    qt = qtg[g % 2]
                    cpa(qt[:, :], Q[:, :], [Q], [qt])
                    for tb in range(4):
                        scp = pp()
                        mm(scp[:, 0:128], qt[:, tb * 128:(tb + 1) * 128], keysT[:, l * 2 + pk, :], True, True, [qt, keysT], [scp])
                        sc = scs[(g * 4 + tb) % 2]
                        cpa(sc[:, :], scp[:, 0:128], [scp], [sc])
                        S.emit("dve", lambda e, sc=sc, tb=tb, g=g: e.max(out=tv[:, tb, g, 0:8], in_=sc[:, :]), reads=[sc], writes=[tv])
                        S.emit("dve", lambda e, sc=sc, tb=tb, g=g: e.max_index(out=ti[:, tb, g, 0:8], in_max=tv[:, tb, g, 0:8], in_values=sc[:, :]),
                               reads=[sc, tv], writes=[ti])
                        S.emit("dve", lambda e, sc=sc, tb=tb, g=g: e.match_replace(out=sc2[:, :], in_to_replace=tv[:, tb, g, 0:8], in_values=sc[:, :],
                                                                                imm_value=-1e30), reads=[sc, tv], writes=[sc2])
                        S.emit("dve", lambda e, tb=tb, g=g: e.max(out=tv[:, tb, g, 8:16], in_=sc2[:, :]), reads=[sc2], writes=[tv])
                        S.emit("dve", lambda e, tb=tb, g=g: e.max_index(out=ti[:, tb, g, 8:16], in_max=tv[:, tb, g, 8:16], in_values=sc2[:, :]),
                               reads=[sc2, tv], writes=[ti])
            for tb in range(4):
                cp("dve", tif[:, :, :], ti[:, tb, :, :], [ti], [tif])
                for h in range(8):
                    in0 = tv[:, tb, 2 * h, :].unsqueeze(2).to_broadcast([128, 16, 16])
                    in1 = tv[:, tb, 2 * h + 1, :].unsqueeze(1).to_broadcast([128, 16, 16])
                    tt(cand[:, :].rearrange("p (a b) -> p a b", a=16), in0, in1, ALU.add, [tv], [cand])
                    S.emit("dve", lambda e, h=h: e.max(out=bs[:, h, 0:8], in_=cand[:, :]), reads=[cand], writes=[bs])
                    S.emit("dve", lambda e, h=h: e.max_index(out=bc[:, h, 0:8], in_max=bs[:, h, 0:8], in_values=cand[:, :]),
                           reads=[cand, bs], writes=[bc])
                    S.emit("dve", lambda e, h=h: e.match_replace(out=cand2[:, :], in_to_replace=bs[:, h, 0:8], in_values=cand[:, :],
                                                                 imm_value=-1e30), reads=[cand, bs], writes=[cand2])
                    S.emit("dve", lambda e, h=h: e.max(out=bs[:, h, 8:16], in_=cand2[:, :]), reads=[cand2], writes=[bs])
                    S.emit("dve", lambda e, h=h: e.max_index(out=bc[:, h, 8:16], in_max=bs[:, h, 8:16], in_values=cand2[:, :]),
                           reads=[cand2, bs], writes=[bc])
                bcf = bc[:, :, :].rearrange("p h k -> p (h k)")
                S.emit("dve", lambda e, bcf=bcf: e.tensor_single_scalar(out=au[:, :], in_=bcf, scalar=4, op=ALU.arith_shift_right),
                       reads=[bc], writes=[au])
                S.emit("dve", lambda e, bcf=bcf: e.tensor_single_scalar(out=bu[:, :], in_=bcf, scalar=15, op=ALU.bitwise_and),
                       reads=[bc], writes=[bu])
                cp("dve", af[:, :, :].rearrange("p h k -> p (h k)"), au[:, :], [au], [af])
                cp("dve", bf[:, :, :].rearrange("p h k -> p (h k)"), bu[:, :], [bu], [bf])
                eq8 = acc[:, :].bitcast(BF16).rearrange("p (h k a) -> p h k a", h=8, k=16)
                tif4 = tif[:, :, :].rearrange("p (h q) a -> p h q a", q=2)
                for (sel, pq_, dst) in ((af, 0, i1), (bf, 1, i2)):
                    tt(eq8, io16[:, :].unsqueeze(1).unsqueeze(1).to_broadcast([128, 8, 16, 16]),
                       sel[:, :, :].unsqueeze(3).to_broadcast([128, 8, 16, 16]), ALU.is_equal, [io16, sel], [acc])
                    tt(eq8, eq8, tif4[:, :, pq_, :].unsqueeze(2).to_broadcast([128, 8, 16, 16]), ALU.mult, [acc, tif], [acc])
                    red(dst[:, :, :], eq8, ALU.add, [acc], [dst])
                tt(gw[:, :, :], bs[:, :, :], bs[:, :, 0:1].to_broadcast([128, 8, 16]), ALU.subtract, [bs], [gw])
                act(gw[:, :, :], gw[:, :, :], AF.Exp, [gw], [gw])
                red(gs8[:, :], gw[:, :, :], ALU.add, [gw], [gs8])
                recip(gs8[:, :], gs8[:, :], [gs8], [gs8])
                tt(gw[:, :, :], gw[:, :, :], gs8[:, :].unsqueeze(2).to_broadcast([128, 8, 16]), ALU.mult, [gw, gs8], [gw])
                for (srcb, dstb) in ((i1, i1T), (i2, i2T), (gw, gT)):
                    bank = pp()
                    tr(bank[:, 0:128], srcb[:, :, :].rearrange("p h k -> p (h k)"), ident[:, :], [srcb, ident], [bank])
                    cpa(dstb[:, tb * 128:(tb + 1) * 128], bank[:, 0:128], [bank], [dstb])
            if st == 0 and l == 0:
                dump("i1T", i1T, i1T[:, 0:512], [128, 512])
                dump("i2T", i2T, i2T[:, 0:512], [128, 512])
                dump("gT", gT, gT[:, 0:512], [128, 512])

            def accap(c):
                if c < 4:
                    return pbanks[c // 2], pbanks[c // 2][:, (c % 2) * 256:(c % 2) * 256 + 256]
                return pbig, pbig[:, (c - 4) * 256:(c - 4) * 256 + 256]

            for hf in range(2):
                ppbase[0] = 0
                for sbk in range(64):
                    t0 = hf * 256 + sbk * 4
                    O2, O1 = ohA[sbk % 2], ohB[sbk % 2]
                    iob = io128[:, :].unsqueeze(1).to_broadcast([128, 4, 128])
                    tt(O2[:, :, :], iob, i2T[:, t0:t0 + 4].unsqueeze(2).to_broadcast([128, 4, 128]), ALU.is_equal,
                       [io128, i2T], [O2])
                    tt(O1[:, :, :], iob, i1T[:, t0:t0 + 4].unsqueeze(2).to_broadcast([128, 4, 128]), ALU.is_equal,
                       [io128, i1T], [O1])
                    tt(O1[:, :, :], O1[:, :, :], gT[:, t0:t0 + 4].unsqueeze(2).to_broadcast([128, 4, 128]), ALU.mult,
                       [O1, gT], [O1], eng="pool")
                    bank = pp()
                    for tq in range(4):
                        mm(bank[:, tq * 128:(tq + 1) * 128], O2[:, tq, :], O1[:, tq, :], True, True, [O2, O1], [bank])
                    tl = sbk * 4
                    outv = S3[:, tl:tl + 4, :]
                    inv = bank[:, :].rearrange("p (t i) -> p t i", t=4)
                    cpa(outv, inv, [bank], SALL)
                ppbase[0] = 2
                LAG = 2
                tinfo = {}

                def emitH(i1v):
                    wb = wget((l, "ud", i1v))
                    UTv = wb[:, 0:1024].rearrange("p (k n) -> p k n", k=8)
                    Vv = wb[:, 1024:2048]
                    Hb = pp()
                    for kc in range(8):
                        mm(Hb[:, 0:256], UTv[:, kc, :], xnT[:, kc, hf * 256:(hf + 1) * 256], kc == 0, kc == 7, [wb, xnT], [Hb])
                    geB, GtB = (Bt[1], xcb)[i1v % 2], Bt[2 + i1v % 3]
                    gev = geB[:, 0:128].bitcast(BF16) if i1v % 2 == 0 else xcb[:, 0, 0:256]
                    Gtv = GtB[:, 0:128].bitcast(BF16)
                    act(gev, Hb[:, 0:256], AF.Gelu_apprx_tanh, [Hb], [geB])
                    tt(Gtv, gev, S3[:, :, i1v], ALU.mult, [geB] + SALL, [GtB])
                    tinfo[i1v] = (wb, Vv, GtB, Gtv)

                def emitO(i1v):
                    wb, Vv, GtB, Gtv = tinfo.pop(i1v)
                    for c in range(8):
                        ab, aap = accap(c)
                        mm(aap, Vv[:, c * 128:(c + 1) * 128], Gtv, i1v == 0 and c % 2 == 0, i1v == 127, [wb, GtB], [ab])

                for i in range(128 + LAG):
                    if i < 128:
                        emitH(i)
                    if i >= LAG:
                        emitO(i - LAG)
                for c in range(8):
                    ab, aap = accap(c)
                    tt(hT[:, c, hf * 256:(hf + 1) * 256], hT[:, c, hf * 256:(hf + 1) * 256], aap, ALU.add, [hT, ab], [hT])
                ppbase[0] = 0

        def ple(st, l):
            norm(l, 2, False)
            ptok = acc
            ptv_ = acc[:, :].rearrange("p (b d) -> p b d", b=4)
            dma("sp", ptv_, p_d[l, st * TS:(st + 1) * TS, :].rearrange("(b p) d -> p b d", p=128), dma_out=ptok)
            for kc in range(2):
                bank = pp()
                for tb in range(4):
                    tr(bank[:, tb * 128:(tb + 1) * 128], ptv_[:, tb, kc * 128:(kc + 1) * 128], ident[:, :], [ptok, ident], [bank])
                cpa(pT[:, kc, :], bank[:, :], [bank], [pT])
            for jb in range(2):
                wgb = wget((l, "pg", jb))
                wpb = wget((l, "pp", jb))
                wgv, wpv = w3(wgb, 512), w3(wpb, 512, kc=2)
                for jj in range(4):
                    j = 4 * jb + jj
                    G, Pj = pp(), pp()
                    proj(G, wgv, jj * 128, xnT, [wgb, xnT])
                    proj(Pj, wpv, jj * 128, pT, [wpb, pT], kcs=2)
                    B1 = Bt[j % 5]
                    sigmoid_from(G[:, :], B1, [G])
                    tt(B1[:, 0:TS], B1[:, 0:TS], Pj[:, :], ALU.mult, [B1, Pj], [B1])
                    tt(hT[:, j, :], hT[:, j, :], B1[:, 0:TS], ALU.add, [hT, B1], [hT])

        def final(st):
            norm(L, 0, True)
            for tb in range(4):
                for c in range(8):
                    tr(pbig[:, c * 128:(c + 1) * 128], big[:, c, tb * 128:(tb + 1) * 128], ident[:, :], [big, ident], [pbig])
                cpa(xtok[:, :], pbig[:, :], [pbig], [xtok])
                r0 = st * TS + tb * 128
                dma("sp", o_d[r0:r0 + 128, :], xtok[:, :], dma_in=xtok)

        def dump_h(name):
            dump(name, hT, hT[:, :, :], [128, 8, TS])

        for st in range(NST):
            load_x(st)
            if st == 0:
                dump_h("h_in")
            for l in range(L):
                mixer(st, l)
                if st == 0:
                    ut_prep(l)
                if st == 0 and l == 0:
                    dump("xn", xnT, xnT[:, :, :], [128, 8, TS], BF16)
                    dump("attn", attnT, attnT[:, :, :], [128, 8, TS], BF16)
                    dump("rec", recT, recT[:, :, :], [128, 8, TS], BF16)
                    dump("mrg", cvm, cvm[:, :, :], [128, 8, TS], BF16)
                    dump_h("h_mix")
                peer(st, l)
                if st == 0 and l == 0:
                    dump_h("h_peer")
                ple(st, l)
                if st == 0 and l == 0:
                    dump_h("h_ple")
            final(st)
        S.wait_all("sp", [xtok, hT, big, cvm, attnT, recT, xnT, qtg[0], qtg[1], Bt[0], Bt[1], Bt[2], Bt[3], Bt[4], xr, xc, dcol])
        return wrec

    worder = prog(DrySched(), None)
    pidx[0] = 0
    ppbase[0] = 0
    Srec = Sched(nc, es, needed=None)
    prog(Srec, worder)
    for b_ in Buf.ALL:
        b_.reset()
    pidx[0] = 0
    ppbase[0] = 0
    S = Sched(nc, es, needed=Srec.targets)
    prog(S, worder)
    S.flush()
    es.close()
    return nc, S, dbg_outs


def prep_shared(inp):
    inp = {k: np.asarray(v) for k, v in inp.items()}
    sh = {}
    sh["wts"] = host_weights(inp)
    sh["cols"] = host_cols(inp)
    sh["sinks"] = np.ascontiguousarray(np.broadcast_to(inp["attn_sinks"].reshape(1, L * 16), (128, L * 16))).astype(np.float32)
    kT = inp["peer_sub_keys"].transpose(3, 0, 1, 2)
    sh["keysT"] = np.ascontiguousarray(kT).reshape(128, L * 2 * 128).astype(np.float32)
    sh["pu"] = np.ascontiguousarray(inp["peer_u"].reshape(L * NEXP, D))
    sh["pv"] = np.ascontiguousarray(inp["peer_v"].reshape(L * NEXP, D))
    return sh


def kernel(**inputs):
    x = np.asarray(inputs["x"])
    p = np.asarray(inputs["p"])
    B, S_LEN, _ = x.shape
    sh = prep_shared(inputs)
    nc, _, _ = build_nc(S_LEN)
    in_maps = []
    for b in range(B):
        m = dict(sh)
        m["x"] = np.ascontiguousarray(x[b])
        m["p"] = np.ascontiguousarray(p[:, b])
        in_maps.append(m)
    res = run_bass_kernel_spmd(nc, in_maps, core_ids=list(range(B)))
    return np.stack([res.results[b]["out"] for b in range(B)], 0).astype(np.float32)
```
